# Optimizing a Trainium2 kernel written in Bass

```python
import math
import jax, jax.numpy as jnp
from jax import lax
import numpy as np

D_MODEL = 2048
BATCH = 2
SEQ = 4096
DEPTH = 2

MIX_WIDTH = D_MODEL
GLA_HEADS = 8
GLA_DV = (MIX_WIDTH // 2) // GLA_HEADS
GLA_DK = GLA_DV // 2
GLA_GATE_RANK = 16
GLA_GATE_TAU = 16.0
GLA_CHUNK = 64
SB_HEADS = 8
SB_DH = (MIX_WIDTH // 2) // SB_HEADS
SB_BLOCK = 128
D_FF = 4 * D_MODEL
EPS = 1e-6

COL_SIZES = (
    GLA_HEADS * GLA_DK,
    GLA_HEADS * GLA_DK,
    GLA_HEADS * GLA_DV,
    GLA_HEADS * GLA_DV,
    GLA_GATE_RANK,
    SB_HEADS * SB_DH,
    SB_HEADS * SB_DH,
    SB_HEADS * SB_DH,
)
D_IN = sum(COL_SIZES)
COL_SPLITS = tuple(int(v) for v in np.cumsum(COL_SIZES)[:-1])

kernel_name = "hybrid_gla_stickbreaking_parallel_heads"


def rms_norm(x, g):
    xf = x.astype(jnp.float32)
    y = xf * lax.rsqrt(jnp.mean(xf * xf, axis=-1, keepdims=True) + EPS)
    return (y * g.astype(jnp.float32)).astype(x.dtype)


def split_heads(t, n_heads):
    b, s, _ = t.shape
    return t.reshape(b, s, n_heads, -1).transpose(0, 2, 1, 3)


def gla_chunked(q, k, v, log_a):
    b_, h_, t_, dk = q.shape
    dv = v.shape[-1]
    n = t_ // GLA_CHUNK
    f32 = jnp.float32
    q = q.astype(f32).reshape(b_, h_, n, GLA_CHUNK, dk) * (dk ** -0.5)
    k = k.astype(f32).reshape(b_, h_, n, GLA_CHUNK, dk)
    v = v.astype(f32).reshape(b_, h_, n, GLA_CHUNK, dv)
    g = log_a.astype(f32).reshape(b_, h_, n, GLA_CHUNK, dk)
    bcum = jnp.cumsum(g, axis=3)
    b_last = bcum[:, :, :, -1:, :]
    q_e = q * jnp.exp(bcum)
    k_e = k * jnp.exp(-bcum)
    causal = jnp.tril(jnp.ones((GLA_CHUNK, GLA_CHUNK), dtype=bool))
    scores = jnp.einsum('bhncd,bhnsd->bhncs', q_e, k_e)
    scores = jnp.where(causal, scores, 0.0)
    o_intra = jnp.einsum('bhncs,bhnse->bhnce', scores, v)
    k_dec = k * jnp.exp(b_last - bcum)
    u = jnp.einsum('bhncd,bhnce->bhnde', k_dec, v)
    decay = jnp.exp(b_last[:, :, :, 0, :])

    def step(state, inp):
        d, uu = inp
        return d[..., None] * state + uu, state

    s0 = jnp.zeros((b_, h_, dk, dv), f32)
    _, s_prev = lax.scan(step, s0, (jnp.moveaxis(decay, 2, 0), jnp.moveaxis(u, 2, 0)))
    s_prev = jnp.moveaxis(s_prev, 0, 2)
    o_inter = jnp.einsum('bhncd,bhnde->bhnce', q_e, s_prev)
    return (o_intra + o_inter).reshape(b_, h_, t_, dv)


def stick_breaking_attention(q, k, v):
    b_, h_, t_, d = q.shape
    n_blocks = t_ // SB_BLOCK
    qf = q.astype(jnp.float32)
    kf = k.astype(jnp.float32)
    vf = v.astype(jnp.float32)
    scale = 1.0 / math.sqrt(d)
    key_pos = jnp.arange(t_)

    def one_block(i):
        start = i * SB_BLOCK
        qb = lax.dynamic_slice_in_dim(qf, start, SB_BLOCK, axis=2)
        z = jnp.einsum('bhqd,bhkd->bhqk', qb, kf) * scale
        q_pos = start + jnp.arange(SB_BLOCK)
        mask = key_pos[None, :] < q_pos[:, None]
        log_beta = jax.nn.log_sigmoid(z)
        log_1m = jnp.where(mask, jax.nn.log_sigmoid(-z), 0.0)
        suffix = lax.cumsum(log_1m, axis=3, reverse=True) - log_1m
        a = jnp.where(mask, jnp.exp(log_beta + suffix), 0.0)
        return jnp.einsum('bhqk,bhkd->bhqd', a, vf)

    out = lax.map(one_block, jnp.arange(n_blocks))
    return jnp.moveaxis(out, 0, 2).reshape(b_, h_, t_, d)


def hybrid_layer(x, attn_norm, w_in, w_gate_up, b_gate, gla_out_norm,
                 sb_q_norm, sb_k_norm, sb_out_norm, w_o, mlp_norm, w_up, w_down):
    b_, s_, _ = x.shape
    h = rms_norm(x, attn_norm)
    proj = jnp.einsum('bsd,de->bse', h, w_in)
    g_q, g_k, g_v, g_gate, g_lr, s_q, s_k, s_v = jnp.split(proj, COL_SPLITS, axis=-1)

    gate_logits = jnp.einsum('bsr,re->bse', g_lr, w_gate_up) + b_gate
    log_a = jax.nn.log_sigmoid(gate_logits.astype(jnp.float32)) / GLA_GATE_TAU
    o_gla = gla_chunked(split_heads(g_q, GLA_HEADS), split_heads(g_k, GLA_HEADS),
                        split_heads(g_v, GLA_HEADS), split_heads(log_a, GLA_HEADS))
    o_gla = rms_norm(o_gla.transpose(0, 2, 1, 3), gla_out_norm).astype(x.dtype)
    o_gla = o_gla * jax.nn.silu(g_gate.reshape(b_, s_, GLA_HEADS, GLA_DV))
    o_gla = o_gla.reshape(b_, s_, GLA_HEADS * GLA_DV)

    sq = rms_norm(split_heads(s_q, SB_HEADS), sb_q_norm)
    sk = rms_norm(split_heads(s_k, SB_HEADS), sb_k_norm)
    o_sb = stick_breaking_attention(sq, sk, split_heads(s_v, SB_HEADS))
    o_sb = rms_norm(o_sb.transpose(0, 2, 1, 3), sb_out_norm).astype(x.dtype)
    o_sb = o_sb.reshape(b_, s_, SB_HEADS * SB_DH)

    mixed = jnp.concatenate([o_gla, o_sb], axis=-1)
    x = x + jnp.einsum('bse,ed->bsd', mixed, w_o)

    hm = rms_norm(x, mlp_norm)
    up = jnp.square(jax.nn.relu(jnp.einsum('bsd,df->bsf', hm, w_up)))
    return x + jnp.einsum('bsf,fd->bsd', up, w_down)


def setup_inputs(seed: int = 0) -> dict:
    key = jax.random.key(seed)
    ks = jax.random.split(key, 16)
    f32 = jnp.float32
    nrm = lambda k, shape, s: jax.random.normal(k, shape, f32) * s
    gain = lambda k, shape: 1.0 + 0.02 * jax.random.normal(k, shape, f32)
    return {
        "x": jax.random.normal(ks[0], (BATCH, SEQ, D_MODEL), f32),
        "attn_norm": gain(ks[1], (DEPTH, D_MODEL)),
        "w_in": nrm(ks[2], (DEPTH, D_MODEL, D_IN), D_MODEL ** -0.5),
        "w_gate_up": nrm(ks[3], (DEPTH, GLA_GATE_RANK, GLA_HEADS * GLA_DK), GLA_GATE_RANK ** -0.5),
        "b_gate": nrm(ks[4], (DEPTH, GLA_HEADS * GLA_DK), 0.1),
        "gla_out_norm": gain(ks[5], (DEPTH, GLA_DV)),
        "sb_q_norm": gain(ks[6], (DEPTH, SB_DH)),
        "sb_k_norm": gain(ks[7], (DEPTH, SB_DH)),
        "sb_out_norm": gain(ks[8], (DEPTH, SB_DH)),
        "w_o": nrm(ks[9], (DEPTH, MIX_WIDTH, D_MODEL), MIX_WIDTH ** -0.5),
        "mlp_norm": gain(ks[10], (DEPTH, D_MODEL)),
        "w_up": nrm(ks[11], (DEPTH, D_MODEL, D_FF), D_MODEL ** -0.5),
        "w_down": nrm(ks[12], (DEPTH, D_FF, D_MODEL), D_FF ** -0.5),
    }


def reference(x, attn_norm, w_in, w_gate_up, b_gate, gla_out_norm, sb_q_norm,
              sb_k_norm, sb_out_norm, w_o, mlp_norm, w_up, w_down):
    for l in range(DEPTH):
        x = hybrid_layer(x, attn_norm[l], w_in[l], w_gate_up[l], b_gate[l], gla_out_norm[l],
                         sb_q_norm[l], sb_k_norm[l], sb_out_norm[l], w_o[l],
                         mlp_norm[l], w_up[l], w_down[l])
    return x
```

```python
import os
import numpy as np
import ml_dtypes
from contextlib import ExitStack
import concourse.bass as bass
import concourse.mybir as mybir
from concourse.bass_utils import run_bass_kernel_spmd

F32 = mybir.dt.float32
BF16 = mybir.dt.bfloat16
AF = mybir.ActivationFunctionType
ALU = mybir.AluOpType

NCORES = 8
D = 2048
DC = 16
TL = 1024
SEQ = 4096
DFF = 8192
EPS = 1e-6
DEPTH = 2


class Sched:
    def __init__(self, nc, es):
        self.nc = nc
        self.es = es
        self.engs = ['pe', 'act', 'dve', 'pool', 'sp']
        self.ops = {e: [] for e in self.engs}
        self.lastw = {}
        self.readers = {}
        self.dsem_cnt = {}
        self.dsems = {}
        self.esem = {e: es.enter_context(nc.semaphore("sem_" + e)) for e in self.engs}
        self.waited = {e: {} for e in self.engs}

    def _dep(self, eng, ev, waits):
        if ev[0] == 'e':
            if ev[1] == eng and eng in ('pe', 'sp'):
                return
            k = ('e', ev[1])
        else:
            k = ('d', ev[1])
        v = ev[2]
        if self.waited[eng].get(k, -1) >= v:
            return
        self.waited[eng][k] = v
        waits.append(ev)

    def add(self, eng, fn, reads=(), writes=(), dma=None, inc=16):
        cand = {}

        def c(ev):
            if ev is None:
                return
            k = (ev[0], ev[1])
            if k not in cand or cand[k][2] < ev[2]:
                cand[k] = ev
        for t in reads:
            c(self.lastw.get(t))
        for t in writes:
            c(self.lastw.get(t))
            for r in self.readers.get(t, ()):
                c(r)
        waits = []
        for ev in cand.values():
            self._dep(eng, ev, waits)
        idx = len(self.ops[eng])
        if dma is not None:
            if dma not in self.dsems:
                self.dsems[dma] = self.es.enter_context(self.nc.semaphore("d_%d" % len(self.dsems)))
                self.dsem_cnt[dma] = 0
            self.dsem_cnt[dma] += inc
            ev = ('d', dma, self.dsem_cnt[dma])
        else:
            ev = ('e', eng, idx)
        self.ops[eng].append(dict(fn=fn, waits=waits, dma=dma, inc=inc))
        for t in reads:
            self.readers.setdefault(t, []).append(ev)
        for t in writes:
            self.lastw[t] = ev
            self.readers[t] = []
        return ev

    def barrier(self, skip=(), keep=None):
        evs = []
        for e in self.engs:
            for i in range(len(self.ops[e]) - 1, -1, -1):
                op = self.ops[e][i]
                if op['dma'] is None and op['fn'] is not None:
                    evs.append(('e', e, i))
                    break
        for k, c in self.dsem_cnt.items():
            if k not in skip:
                evs.append(('d', k, c))
        for e in self.engs:
            waits = []
            for ev in evs:
                if ev[0] == 'e' and ev[1] == e:
                    continue
                self._dep(e, ev, waits)
            self.ops[e].append(dict(fn=None, waits=waits, dma=None, inc=16))
        kept = {t: w for t, w in self.lastw.items() if keep is not None and keep(t)}
        self.lastw.clear()
        self.readers.clear()
        self.lastw.update(kept)

    def emit(self):
        needed = {e: set() for e in self.engs}
        for e in self.engs:
            for op in self.ops[e]:
                for ev in op['waits']:
                    if ev[0] == 'e':
                        needed[ev[1]].add(ev[2])
        sig = {}
        for e in self.engs:
            c = 0
            for i in range(len(self.ops[e])):
                if i in needed[e]:
                    c += 1
                    sig[(e, i)] = c
        S = self

        def run(ename, eng):
            for i, op in enumerate(S.ops[ename]):
                for ev in op['waits']:
                    if ev[0] == 'e':
                        eng.wait_ge(S.esem[ev[1]], sig[(ev[1], ev[2])])
                    else:
                        eng.wait_ge(S.dsems[ev[1]], ev[2])
                if op['fn'] is None:
                    continue
                inst = op['fn'](eng)
                if op['dma'] is not None:
                    if op['inc'] == 1:
                        inst.then_inc(S.dsems[op['dma']])
                    else:
                        inst.then_inc(S.dsems[op['dma']], op['inc'])
                elif (ename, i) in sig:
                    inst.then_inc(S.esem[ename], 1)

        with self.nc.Block() as block:
            @block.tensor
            def _(pe):
                run('pe', pe)

            @block.scalar
            def _(act):
                run('act', act)

            @block.vector
            def _(dve):
                run('dve', dve)

            @block.gpsimd
            def _(pool):
                run('pool', pool)

            @block.sync
            def _(sp):
                run('sp', sp)


class Ctx:
    pass


def carve(arena, off, shape, dt):
    n = 1
    for s in shape[1:]:
        n *= s
    esz = 4 if dt == F32 else 2
    nbytes = n * esz
    assert off % 4 == 0 and nbytes % 4 == 0
    v = arena[0:shape[0], off // 4:(off + nbytes) // 4]
    if dt != F32:
        v = v.bitcast(dt)
    if len(shape) == 3:
        v = v.rearrange("p (a b) -> p a b", b=shape[2])
    elif len(shape) == 4:
        v = v.rearrange("p (a b c) -> p a b c", b=shape[2], c=shape[3])
    return v, off + nbytes


def phase_norm(S, C, gcol, tag):
    xT, hbuf = C.xT, C.hbuf
    for half in range(2):
        sl = slice(half * 512, (half + 1) * 512)
        ps = C.psum[0]
        for dc in range(DC):
            sq = C.sqv[dc % 2]
            S.add('act', lambda e, dc=dc, sq=sq, sl=sl: e.activation(out=sq[:, :], in_=xT[:, dc, sl], func=AF.Square),
                  reads=[('xT', dc, half)], writes=[('sqv', dc % 2)])
            S.add('pe', lambda e, dc=dc, sq=sq, ps=ps: e.matmul(ps[:, :], lhsT=C.onesD[:, :], rhs=sq[:, :],
                                                               start=(dc == 0), stop=(dc == DC - 1)),
                  reads=[('sqv', dc % 2)], writes=['ps0'])
        S.add('act', lambda e, ps=ps: e.activation(out=C.lnv[:, :], in_=ps[:, :], func=AF.Ln, bias=C.epsc[:, 0:1], scale=1.0),
              reads=['ps0'], writes=['lnv'])
        S.add('act', lambda e: e.activation(out=C.rstd[:, :], in_=C.lnv[:, :], func=AF.Exp, scale=-0.5),
              reads=['lnv'], writes=['rstd'])
        for dc in range(DC):
            S.add('dve', lambda e, dc=dc, sl=sl: e.scalar_tensor_tensor(
                out=hbuf[:, dc, sl], in0=xT[:, dc, sl], scalar=gcol[:, dc:dc + 1], in1=C.rstd[:, :],
                op0=ALU.mult, op1=ALU.mult),
                reads=[('xT', dc, half), 'rstd'], writes=[('hbuf', dc, half)])


def load_wtile(S, C, w_ap, k):
    ss = k % len(C.stg)
    sb = k % 3
    stg = C.stg[ss]
    wb = C.wb[sb]
    S.add('sp', lambda e: e.dma_start(out=stg[:, :], in_=w_ap), writes=[('stg', ss)], dma=('stg', ss))
    if k % 2 == 0:
        S.add('act', lambda e: e.activation(out=wb[:, :], in_=stg[:, :], func=AF.Copy), reads=[('stg', ss)], writes=[('wb', sb)])
    else:
        S.add('dve', lambda e: e.tensor_copy(out=wb[:, :], in_=stg[:, :]), reads=[('stg', ss)], writes=[('wb', sb)])
    return wb.rearrange("p (a b) -> p a b", b=128), ('wb', sb)


def phase_wo(S, C, wo_d):
    for oc in range(DC):
        wt, wtok = load_wtile(S, C, wo_d[oc, :, :], C.wk)
        C.wk += 1
        for half in range(2):
            sl = slice(half * 512, (half + 1) * 512)
            b = C.pk % 4
            C.pk += 1
            ps = C.psum[b]
            for ic in range(DC):
                S.add('pe', lambda e, ic=ic, ps=ps, wt=wt, sl=sl: e.matmul(ps[:, :], lhsT=wt[:, ic, :], rhs=C.hbuf[:, ic, sl],
                                                                         start=(ic == 0), stop=(ic == DC - 1)),
                      reads=[wtok, ('hbufq', ic // 4, 2 * half), ('hbufq', ic // 4, 2 * half + 1)], writes=[('ps', b)])
            S.add('dve', lambda e, oc=oc, ps=ps, sl=sl: e.tensor_tensor(out=C.xT[:, oc, sl], in0=C.xT[:, oc, sl], in1=ps[:, :], op=ALU.add),
                  reads=[('ps', b)], writes=[('xT', oc, half)])


def phase_mlp(S, C, wup_d, wdn_d):
    for fg in range(4):
        for fc in range(16):
            wt, wtok = load_wtile(S, C, wup_d[fg * 16 + fc, :, :], C.wk)
            C.wk += 1
            for half in range(2):
                sl = slice(half * 512, (half + 1) * 512)
                b = C.pk % 4
                C.pk += 1
                ps = C.psum[b]
                for dc in range(DC):
                    S.add('pe', lambda e, dc=dc, ps=ps, wt=wt, sl=sl: e.matmul(ps[:, :], lhsT=wt[:, dc, :], rhs=C.hbuf[:, dc, sl],
                                                                             start=(dc == 0), stop=(dc == DC - 1)),
                          reads=[wtok, ('hbuf', dc, half)], writes=[('ps', b)])
                r = C.rk % 2
                C.rk += 1
                rt = C.relu[r]
                S.add('act', lambda e, ps=ps, rt=rt: e.activation(out=rt[:, :], in_=ps[:, :], func=AF.Relu),
                      reads=[('ps', b)], writes=[('relu', r)])
                S.add('pool', lambda e, rt=rt, fc=fc, sl=sl: e.tensor_tensor(out=C.upT[:, fc, sl], in0=rt[:, :], in1=rt[:, :], op=ALU.mult),
                      reads=[('relu', r)], writes=[('upT', fc, half)])
        for dc in range(DC):
            wt, wtok = load_wtile(S, C, wdn_d[fg, dc, :, :], C.wk)
            C.wk += 1
            for half in range(2):
                sl = slice(half * 512, (half + 1) * 512)
                b = 4 + C.pk2 % 4
                C.pk2 += 1
                ps = C.psum[b]
                for fc in range(16):
                    S.add('pe', lambda e, fc=fc, ps=ps, wt=wt, sl=sl: e.matmul(ps[:, :], lhsT=wt[:, fc, :], rhs=C.upT[:, fc, sl],
                                                                             start=(fc == 0), stop=(fc == 15)),
                          reads=[wtok, ('upT', fc, half)], writes=[('ps', b)])
                S.add('dve', lambda e, dc=dc, ps=ps, sl=sl: e.tensor_tensor(out=C.xT[:, dc, sl], in0=C.xT[:, dc, sl], in1=ps[:, :], op=ALU.add),
                      reads=[('ps', b)], writes=[('xT', dc, half)])


CO_ONESD, CO_ONESH, CO_ONES1, CO_TRINEG = 0, 128, 256, 384
CO_MASKD = 512
CO_IDENT = 2560
CO_TRIG = 2688
CO_TRIA = 2816
CO_MASKG2 = 2944
NCONST = 3200
NEGM = -30000.0


def load_pieces(S, C, w_d, npieces, dst):
    for k in range(npieces):
        ss = k % 2
        stg = C.stg[ss]
        S.add('sp', lambda e, k=k, stg=stg: e.dma_start(out=stg[:, :], in_=w_d[k, :, :]), writes=[('stg', ss)], dma=('stg', ss))
        if k % 2 == 0:
            S.add('act', lambda e, k=k, stg=stg: e.activation(out=dst[:, k, :], in_=stg[:, :], func=AF.Copy), reads=[('stg', ss)], writes=[('wpc', k)])
        else:
            S.add('dve', lambda e, k=k, stg=stg: e.tensor_copy(out=dst[:, k, :], in_=stg[:, :]), reads=[('stg', ss)], writes=[('wpc', k)])


def head_norm(S, C, ps_ap, ps_tok, n, gain_ap, dst_ap, dst_tok, ms_ap, ms_tok, raw=None, raw_tok=None):
    k = C.nk % 2
    C.nk += 1
    sq = C.sqv2[k][:, 0:n]
    lnv = C.lnv2[:, 0:n]
    rstd = C.rstd2[:, 0:n]
    src = ps_ap
    src_tok = ps_tok
    if raw is not None:
        S.add('act', lambda e: e.activation(out=raw, in_=ps_ap, func=AF.Copy), reads=[ps_tok], writes=[raw_tok])
        src, src_tok = raw, raw_tok
    S.add('act', lambda e: e.activation(out=sq, in_=ps_ap, func=AF.Square), reads=[ps_tok], writes=[('sqv2', k)])
    S.add('pe', lambda e: e.matmul(ms_ap, lhsT=C.onesH[:, :], rhs=sq, start=True, stop=True), reads=[('sqv2', k)], writes=[ms_tok])
    S.add('act', lambda e: e.activation(out=lnv, in_=ms_ap, func=AF.Ln, bias=C.epsc[:, 0:1], scale=1.0), reads=[ms_tok], writes=['lnv2'])
    S.add('act', lambda e: e.activation(out=rstd, in_=lnv, func=AF.Exp, scale=-0.5), reads=['lnv2'], writes=['rstd2'])
    S.add('dve', lambda e: e.scalar_tensor_tensor(out=dst_ap, in0=src, scalar=gain_ap, in1=rstd, op0=ALU.mult, op1=ALU.mult),
          reads=[src_tok, 'rstd2'], writes=[dst_tok])


def load_hT(S, C, hall, i):
    q = i // 2
    for hq in range(2):
        r = (i % 2) * 2 + hq
        if isinstance(hall, list):
            src = hall[q][r * D:(r + 1) * D, :]
        else:
            src = hall[r * D:(r + 1) * D, q * 256:(q + 1) * 256]
        S.add('sp', lambda e, hq=hq, src=src: e.dma_start(out=C.hT_t[:, :, hq * 256:(hq + 1) * 256], in_=src.rearrange("(c p) t -> p c t", p=128)),
              reads=[('hall', q)], writes=['hT_t%d' % hq], dma=('hT', hq))


def store_mix(S, C, mixloc, i, mb, row0, reads):
    q = i // 2
    for hq in range(2):
        r = (i % 2) * 2 + hq
        if isinstance(mixloc, list):
            dst = mixloc[q][r * 512 + row0:r * 512 + row0 + 256, :]
        else:
            dst = mixloc[r * 512 + row0:r * 512 + row0 + 256, q * 256:(q + 1) * 256]
        S.add('sp', lambda e, hq=hq, dst=dst: e.dma_start(out=dst.rearrange("(h p) t -> p h t", p=128), in_=C.mixo[mb][:, :, hq * 256:(hq + 1) * 256]),
              reads=reads, writes=[('mixloc', q, row0, i % 2, hq)], dma=('mixo', mb, hq))


def phase_sb(S, C, wsb_d, hall, mixloc, nrm, qgs):
    ps = C.psum
    wv = C.wsb.rearrange("p k (c m) -> p k c m", m=128)
    load_pieces(S, C, wsb_d, 6, C.wsb)
    NT = 8

    def proj_chunks(i):
        bf = i % 2
        hT = C.hT2[bf]
        htoks = [('hT_t', bf, 0), ('hT_t', bf, 1)]
        t0 = i * 512
        out = []
        for pi in range(4):
            def mm(pi=pi):
                for dc in range(DC):
                    S.add('pe', lambda e, dc=dc: e.matmul(ps[6][:, :], lhsT=wv[:, pi, dc, :], rhs=hT[:, dc, :], start=(dc == 0), stop=(dc == DC - 1)),
                          reads=[('wpc', pi)] + htoks, writes=[('ps', 6)])

            def nrmf(pi=pi):
                h = pi % 2
                rk = 0
                if pi < 2:
                    head_norm(S, C, ps[6][:, :], ('ps', 6), 512, qgs, C.qn2[bf][h][:, :], ('qn', bf, h), ps[7][:, :], ('ps', 7),
                              raw=C.raw[rk][:, :], raw_tok=('raw', rk))
                else:
                    head_norm(S, C, ps[6][:, :], ('ps', 6), 512, nrm[:, 2:3], C.KT[h][:, t0:t0 + 512], ('KT', h, i), ps[7][:, :], ('ps', 7),
                              raw=C.raw[rk][:, :], raw_tok=('raw', rk))
            out += [mm, nrmf]
        for s in range(4):
            def vmm(s=s):
                for dc in range(DC):
                    S.add('pe', lambda e, dc=dc: e.matmul(ps[6][:, 0:256], lhsT=hT[:, dc, s * 128:(s + 1) * 128],
                                                         rhs=wv[:, 4:6, dc, :], start=(dc == 0), stop=(dc == DC - 1)),
                          reads=[('wpc', 4), ('wpc', 5)] + htoks, writes=[('ps', 6)])
                blk = i * 4 + s
                S.add('dve', lambda e: e.tensor_copy(out=C.Vall[:, blk, :], in_=ps[6][:, 0:256]), reads=[('ps', 6)], writes=[('V', blk)])
            out.append(vmm)
        return out

    def load(i):
        q = i // 2
        for hq in range(2):
            r = (i % 2) * 2 + hq
            src = hall[q][r * D:(r + 1) * D, :] if isinstance(hall, list) else hall[r * D:(r + 1) * D, q * 256:(q + 1) * 256]
            S.add('sp', lambda e, hq=hq, src=src: e.dma_start(out=C.hT2[i % 2][:, :, hq * 256:(hq + 1) * 256], in_=src.rearrange("(c p) t -> p c t", p=128)),
                  reads=[('hall', q)], writes=[('hT_t', i % 2, hq)], dma=('hT', i % 2, hq))

    load(0)
    for ch in proj_chunks(0):
        ch()
    for i in range(NT):
        bf = i % 2
        qn = C.qn2[bf]
        pending = []
        if i + 1 < NT:
            load(i + 1)
            pending = proj_chunks(i + 1)
        steps = []
        nkb = 4 * i + 4
        for h in range(2):
            for kb in range(nkb - 1, -1, -1):
                steps.append((h, kb, kb == nkb - 1, kb == 0, kb - 4 * i))
        n = len(steps)
        mb = i % 2
        per = -(-len(pending) // n) if pending else 0

        def st_z(t):
            h, kb, first, last, m = steps[t]
            k = t % 2
            qh = qn[h]
            S.add('pe', lambda e: e.matmul(ps[k][:, :], lhsT=C.KT[h][:, kb * 128:(kb + 1) * 128], rhs=qh[:, :], start=True, stop=(m < 0)),
                  reads=[('KT', h, kb // 4), ('qn', bf, h)], writes=[('ps', k)])
            if m >= 0:
                S.add('pe', lambda e: e.matmul(ps[k][:, :], lhsT=C.ident[:, :], rhs=C.maskd[:, m, :], start=False, stop=True), writes=[('ps', k)])
            S.add('act', lambda e: e.activation(out=C.E[k][:, :], in_=ps[k][:, :], func=AF.Exp), reads=[('ps', k)], writes=[('E', k)])
            S.add('act', lambda e: e.activation(out=C.Lp[k][:, :], in_=C.E[k][:, :], func=AF.Ln, bias=C.onec[:, 0:1], scale=1.0),
                  reads=[('E', k)], writes=[('Lp', k)])

        def st_w(t):
            h, kb, first, last, m = steps[t]
            k = t % 2
            qh = qn[h]
            S.add('pe', lambda e: e.matmul(ps[2 + k][:, :], lhsT=C.KT[h][:, kb * 128:(kb + 1) * 128], rhs=qh[:, :], start=True, stop=False),
                  reads=[('KT', h, kb // 4), ('qn', bf, h)], writes=[('ps', 2 + k)])
            if m >= 0:
                S.add('pe', lambda e: e.matmul(ps[2 + k][:, :], lhsT=C.ident[:, :], rhs=C.maskd[:, m, :], start=False, stop=False), writes=[('ps', 2 + k)])
            S.add('pe', lambda e: e.matmul(ps[2 + k][:, :], lhsT=C.triNeg[:, :], rhs=C.Lp[k][:, :], start=False, stop=True),
                  reads=[('Lp', k)], writes=[('ps', 2 + k)])
            S.add('pe', lambda e: e.matmul(ps[4][:, :], lhsT=C.ones1[:, :], rhs=C.Lp[k][:, :], start=True, stop=True),
                  reads=[('Lp', k)], writes=[('ps', 4)])
            if first:
                S.add('dve', lambda e: e.memset(C.carry[:, :], 0.0), writes=['carry'])
            S.add('dve', lambda e: e.tensor_tensor(out=C.Wsb[k][:, :], in0=ps[2 + k][:, :], in1=C.carry[:, :], op=ALU.subtract),
                  reads=[('ps', 2 + k), 'carry'], writes=[('Wsb', k)])
            S.add('dve', lambda e: e.tensor_tensor(out=C.carry[:, :], in0=C.carry[:, :], in1=ps[4][:, :], op=ALU.add),
                  reads=['carry', ('ps', 4)], writes=['carry'])

        def st_a(t):
            k = t % 2
            S.add('act', lambda e: e.activation(out=C.A[k][:, :], in_=C.Wsb[k][:, :], func=AF.Exp), reads=[('Wsb', k)], writes=[('A', k)])

        def st_av(t):
            h, kb, first, last, m = steps[t]
            k = t % 2
            S.add('pe', lambda e: e.matmul(ps[5][:, :], lhsT=C.Vall[:, kb, h * 128:(h + 1) * 128], rhs=C.A[k][:, :], start=first, stop=last),
                  reads=[('V', kb), ('A', k)], writes=[('ps', 5)])
            if last:
                head_norm(S, C, ps[5][:, :], ('ps', 5), 512, nrm[:, 3:4], C.mixo[mb][:, h, :], ('mixo', mb, h), ps[7][:, :], ('ps', 7))

        for t in range(n + 3):
            if t < n:
                st_z(t)
            if 1 <= t <= n:
                st_w(t - 1)
            if 2 <= t <= n + 1:
                st_a(t - 2)
            if t >= 3:
                st_av(t - 3)
            for _ in range(per):
                if pending:
                    pending.pop(0)()
        while pending:
            pending.pop(0)()
        store_mix(S, C, mixloc, i, mb, 256, [('mixo', mb, 0), ('mixo', mb, 1)])


def phase_gla(S, C, wgl_d, hall, mixloc, nrm, wgu, gather=None):
    ps = C.psum
    wv = C.wgl.rearrange("p k (c m) -> p k c m", m=128)
    load_pieces(S, C, wgl_d, 7, C.wgl)
    S.add('dve', lambda e: e.memset(C.lrT[:, :], 1.0), writes=['lrT'])
    S.add('dve', lambda e: e.memset(C.Sf[0][:, :], 0.0), writes=[('Sf', 0)])
    for q in range(6):
        S.add('pool', lambda e, q=q: e.memset(C.Sbh[q][:, :], 0.0), writes=[('Sb', q // 2)])
    for q in range(4):
        S.add('pool', lambda e, q=q: e.memset(C.keh[q][:, :], 0.0), writes=[('ke', q // 2)])
        S.add('pool', lambda e, q=q: e.memset(C.kdc[q][:, :], 0.0), writes=[('kd', q // 2)])
    pj = 0
    sk = 0
    for i in range(int(os.environ.get('G_TILES', '8'))):
        r, c0 = i // 2, (i % 2) * 512
        load_hT(S, C, hall, i)
        mb = i % 2
        for pi, kind in ((0, 'gq'), (3, 'gk'), (1, 'gg0'), (2, 'gg1'), (6, 'lr')):
            b = pj % 2
            pj += 1
            M = 16 if kind == 'lr' else 128
            for dc in range(DC):
                S.add('pe', lambda e, pi=pi, dc=dc, b=b, M=M: e.matmul(ps[b][0:M, :], lhsT=wv[:, pi, dc, 0:M], rhs=C.hT_t[:, dc, :],
                                                                   start=(dc == 0), stop=(dc == DC - 1)),
                      reads=[('wpc', pi), 'hT_t0', 'hT_t1'], writes=[('ps', b)])
            if kind == 'gq':
                S.add('act', lambda e, b=b: e.activation(out=C.gqT[:, :], in_=ps[b][:, :], func=AF.Copy), reads=[('ps', b)], writes=['gqT'])
            elif kind == 'gk':
                S.add('act', lambda e, b=b: e.activation(out=C.gkT[:, :], in_=ps[b][:, :], func=AF.Copy), reads=[('ps', b)], writes=['gkT'])
            elif kind in ('gg0', 'gg1'):
                hh = 0 if kind == 'gg0' else 1
                S.add('act', lambda e, b=b, hh=hh: e.activation(out=C.sg[:, hh, :], in_=ps[b][:, :], func=AF.Silu), reads=[('ps', b)], writes=[('sg', hh)])
            else:
                S.add('act', lambda e, b=b: e.activation(out=C.lrT[0:16, :], in_=ps[b][0:16, :], func=AF.Copy), reads=[('ps', b)], writes=['lrT'])
        GST = int(os.environ.get('G_STAGE', '9'))
        for s in range(4 if GST >= 2 else 0):
            for dc in range(DC):
                S.add('pe', lambda e, dc=dc, s=s: e.matmul(ps[2][:, 0:384], lhsT=C.hT_t[:, dc, s * 128:(s + 1) * 128],
                                                          rhs=wv[:, 3:6, dc, :], start=(dc == 0), stop=(dc == DC - 1)),
                      reads=[('wpc', 3), ('wpc', 4), ('wpc', 5), 'hT_t0', 'hT_t1'], writes=[('ps', 2)])
            S.add('dve', lambda e, s=s: e.tensor_copy(out=C.gv_t[:, s, :], in_=ps[2][:, 128:384]), reads=[('ps', 2)], writes=[('gv_t', s)])
            S.add('dve', lambda e, s=s: e.tensor_copy(out=C.gkt[:, s, :], in_=ps[2][:, 0:128]), reads=[('ps', 2)], writes=[('gkt', s)])
        for s in range(4 if GST >= 3 else 0):
            ts = slice(s * 128, (s + 1) * 128)
            k = s % 2
            en, spv, Eq, Ek, Ed = C.en[k], C.spv[k], C.Eq[k], C.Ek[k], C.Ed[k]
            qe, scm = C.qe[k], C.scm[k]
            kdc = (C.kdc[2 * k], C.kdc[2 * k + 1])
            keh = (C.keh[2 * k], C.keh[2 * k + 1])
            S.add('pe', lambda e, ts=ts: e.matmul(ps[3][:, 0:128], lhsT=C.lrT[0:17, ts], rhs=wgu[0:17, :], start=True, stop=True),
                  reads=['lrT'], writes=[('ps', 3)])
            S.add('act', lambda e, ts=ts, en=en: e.activation(out=en[:, :], in_=ps[3][:, 0:128], func=AF.Exp, scale=-1.0), reads=[('ps', 3)], writes=[('en', k)])
            S.add('act', lambda e, en=en, spv=spv: e.activation(out=spv[:, :], in_=en[:, :], func=AF.Ln, bias=C.onec[:, 0:1], scale=1.0),
                  reads=[('en', k)], writes=[('spv', k)])
            S.add('pe', lambda e, ts=ts, spv=spv: e.matmul(ps[4][:, 0:128], lhsT=spv[:, :], rhs=C.triG[:, :], start=True, stop=True),
                  reads=[('spv', k)], writes=[('ps', 4)])
            S.add('pe', lambda e, ts=ts, spv=spv: e.matmul(ps[4][:, 128:256], lhsT=C.triA[:, :], rhs=spv[:, :], start=True, stop=True),
                  reads=[('spv', k)], writes=[('ps', 4)])
            S.add('act', lambda e, ts=ts, Eq=Eq: e.activation(out=Eq[:, :], in_=ps[4][:, 0:128], func=AF.Exp), reads=[('ps', 4)], writes=[('Eq', k)])
            S.add('act', lambda e, ts=ts, Ek=Ek: e.activation(out=Ek[:, :], in_=ps[4][:, 0:128], func=AF.Exp, scale=-1.0), reads=[('ps', 4)], writes=[('Ek', k)])
            S.add('act', lambda e, ts=ts, Ed=Ed: e.activation(out=Ed[:, :], in_=ps[4][:, 128:256], func=AF.Exp), reads=[('ps', 4)], writes=[('Ed', k)])
            S.add('dve', lambda e, ts=ts, qe=qe, Eq=Eq: e.scalar_tensor_tensor(out=qe[:, :], in0=C.gqT[:, ts], scalar=0.125, in1=Eq[:, :],
                                                                             op0=ALU.mult, op1=ALU.mult),
                  reads=['gqT', ('Eq', k)], writes=[('qe', k)])
            for hh in range(2):
                S.add('dve', lambda e, ts=ts, hh=hh, keh=keh, Ek=Ek: e.tensor_tensor(out=keh[hh][64 * hh:64 * hh + 64, :], in0=C.gkT[64 * hh:64 * hh + 64, ts],
                                                                                in1=Ek[64 * hh:64 * hh + 64, :], op=ALU.mult),
                      reads=['gkT', ('Ek', k)], writes=[('ke', k)])
            for cc in range(2):
                S.add('dve', lambda e, s=s, cc=cc, kdc=kdc, Ed=Ed: e.tensor_tensor(out=kdc[cc][64 * cc:64 * cc + 64, :], in0=C.gkt[64 * cc:64 * cc + 64, s, :],
                                                                              in1=Ed[64 * cc:64 * cc + 64, :], op=ALU.mult),
                      reads=[('gkt', s), ('Ed', k)], writes=[('kd', k)])
            if GST < 4:
                continue
            for hh in range(2):
                S.add('pe', lambda e, hh=hh, keh=keh, qe=qe: e.matmul(ps[5][:, hh * 128:(hh + 1) * 128], lhsT=keh[hh][:, :],
                                                                     rhs=qe[:, :], start=True, stop=True),
                      reads=[('ke', k), ('qe', k)], writes=[('ps', 5)])
            S.add('dve', lambda e, scm=scm: e.tensor_tensor(out=scm[:, :], in0=ps[5][:, 0:256], in1=C.maskG2[:, :], op=ALU.mult),
                  reads=[('ps', 5)], writes=[('scm', k)])
            GSUB = int(os.environ.get('G_SUB', '9'))
            if GSUB < 2:
                continue
            for cc in range(2):
                for hh in range(2):
                    S.add('pe', lambda e, cc=cc, hh=hh, kdc=kdc, s=s: e.matmul(
                        ps[6][:, (cc * 2 + hh) * 128:(cc * 2 + hh + 1) * 128], lhsT=kdc[cc][:, :],
                        rhs=C.gv_t[:, s, hh * 128:(hh + 1) * 128], start=True, stop=True),
                        reads=[('kd', k), ('gv_t', s)], writes=[('ps', 6)])
            s0, s1, s2 = sk % 3, (sk + 1) % 3, (sk + 2) % 3
            sk += 2
            if GSUB < 3:
                continue
            for cc, sa, sb_ in ((0, s0, s1), (1, s1, s2)):
                for hh in range(2):
                    pr = slice(64 * hh, 64 * hh + 64)
                    S.add('dve', lambda e, cc=cc, sa=sa, sb_=sb_, hh=hh, pr=pr, Eq=Eq: e.scalar_tensor_tensor(
                        out=C.Sf[sb_][pr, :], in0=C.Sf[sa][pr, :], scalar=Eq[pr, cc * 64 + 63:cc * 64 + 64],
                        in1=ps[6][pr, (cc * 2 + hh) * 128:(cc * 2 + hh + 1) * 128], op0=ALU.mult, op1=ALU.add),
                        reads=[('Sf', sa), ('Eq', k), ('ps', 6)], writes=[('Sf', sb_)])
                for hh in range(2):
                    S.add('act', lambda e, sb_=sb_, hh=hh: e.activation(out=C.Sbh[2 * sb_ + hh][64 * hh:64 * hh + 64, :], in_=C.Sf[sb_][64 * hh:64 * hh + 64, :], func=AF.Copy),
                          reads=[('Sf', sb_)], writes=[('Sb', sb_)])
            if GST < 5:
                continue
            for hh in range(2):
                S.add('pe', lambda e, hh=hh, s=s, scm=scm: e.matmul(ps[7][:, hh * 128:(hh + 1) * 128], lhsT=C.gv_t[:, s, hh * 128:(hh + 1) * 128],
                                                                    rhs=scm[:, hh * 128:(hh + 1) * 128], start=True, stop=False),
                      reads=[('gv_t', s), ('scm', k)], writes=[('ps', 7)])
                for cc, sx in ((0, s0), (1, s1)):
                    S.add('pe', lambda e, hh=hh, cc=cc, sx=sx, qe=qe: e.matmul(
                        ps[7][:, hh * 128 + cc * 64:hh * 128 + (cc + 1) * 64], lhsT=C.Sbh[2 * sx + hh][:, :],
                        rhs=qe[:, cc * 64:(cc + 1) * 64], start=False, stop=(cc == 1)),
                        reads=[('Sb', sx), ('qe', k)], writes=[('ps', 7)])
            S.add('act', lambda e: e.activation(out=C.osq2[:, :], in_=ps[7][:, 0:256], func=AF.Square), reads=[('ps', 7)], writes=['osq2'])
            S.add('pe', lambda e: e.matmul(ps[5][:, 256:512], lhsT=C.onesH[:, :], rhs=C.osq2[:, :], start=True, stop=True),
                  reads=['osq2'], writes=[('ps', 5)])
            S.add('act', lambda e: e.activation(out=C.lnv3[:, :], in_=ps[5][:, 256:512], func=AF.Ln, bias=C.epsc[:, 0:1], scale=1.0),
                  reads=[('ps', 5)], writes=['lnv3'])
            S.add('act', lambda e: e.activation(out=C.rstd3[:, :], in_=C.lnv3[:, :], func=AF.Exp, scale=-0.5), reads=['lnv3'], writes=['rstd3'])
            S.add('dve', lambda e: e.scalar_tensor_tensor(out=C.otmp[:, :], in0=ps[7][:, 0:256], scalar=nrm[:, 0:1], in1=C.rstd3[:, :],
                                                          op0=ALU.mult, op1=ALU.mult),
                  reads=[('ps', 7), 'rstd3'], writes=['otmp'])
            if GST < 6:
                continue
            S.add('pool', lambda e, ts=ts, mb=mb: e.tensor_tensor(out=C.mixo[mb][:, :, ts], in0=C.otmp.rearrange("p (h c) -> p h c", h=2),
                                                                in1=C.sg[:, :, ts], op=ALU.mult),
                  reads=['otmp', ('sg', 0), ('sg', 1)], writes=[('mixo', mb, s)])
        store_mix(S, C, mixloc, i, mb, 0, [('mixo', mb, s) for s in range(4)])
        if gather is not None and i % 2 == 1:
            gather(i // 2)


GROUPS = [[0, 1, 2, 3], [4, 5, 6, 7]]


def build_program(plan):
    nc = bass.Bass("TRN2", target_bir_lowering=False)
    es = ExitStack()
    with es:
        S = Sched(nc, es)
        C = Ctx()
        C.wk = C.pk = C.pk2 = C.rk = C.nk = C.rawk = C.ak = 0
        C.cache = {}
        layers = sorted(set(l for _, l in plan))

        def din(name, shape, dt=F32):
            return nc.dram_tensor(name, shape, dt, kind="ExternalInput").ap()

        def dout(name, shape, dt=F32):
            return nc.dram_tensor(name, shape, dt, kind="ExternalOutput").ap()

        def dint(name, shape, dt):
            return nc.dram_tensor(name, shape, dt).ap()

        xT_in = din("xT_in", [D, TL])
        consts_d = din("consts", [128, NCONST])
        W = {}
        for l in layers:
            W[l] = {}
            if ('N1', l) in plan:
                W[l]['g1'] = din("g1_%d" % l, [128, DC])
            if ('MIX', l) in plan or ('SB', l) in plan or ('GLA', l) in plan:
                W[l]['wsb'] = din("wsb_%d" % l, [6, 128, 2048])
                W[l]['wgl'] = din("wgl_%d" % l, [7, 128, 2048])
                W[l]['nrm'] = din("nrm_%d" % l, [128, 4])
                W[l]['wgu'] = din("wgu_%d" % l, [17, 128])
            if ('WO', l) in plan:
                W[l]['wo'] = din("wo_%d" % l, [DC, 128, 2048])
            if ('MLP', l) in plan:
                W[l]['g2'] = din("g2_%d" % l, [128, DC])
                W[l]['wup'] = din("wup_%d" % l, [64, 128, 2048])
                W[l]['wdn'] = din("wdn_%d" % l, [4, DC, 128, 2048])
        HX = {}
        for l in layers:
            n1, mix, wo = ('N1', l) in plan, (('MIX', l) in plan or ('SB', l) in plan or ('GLA', l) in plan), ('WO', l) in plan
            if n1 and mix:
                HX[('hloc', l)] = [dint("hloc_%d_%d" % (l, q), [D, 256], BF16) for q in range(4)]
                HX[('hall', l)] = [dint("hall_%d_%d" % (l, q), [4 * D, 256], BF16) for q in range(4)]
            elif n1:
                HX[('hloc', l)] = dout("h_out_%d" % l, [D, TL], BF16)
            elif mix:
                HX[('hall', l)] = din("hall_%d" % l, [4 * D, TL], BF16)
            if mix and wo:
                HX[('mixloc', l)] = [dint("mixloc_%d_%d" % (l, q), [D, 256], BF16) for q in range(4)]
                HX[('mixall', l)] = [dint("mixall_%d_%d" % (l, q), [4 * D, 256], BF16) for q in range(4)]
            elif mix:
                HX[('mixloc', l)] = dout("mix_out_%d" % l, [D, TL], BF16)
            elif wo:
                HX[('mixin', l)] = din("mix_in_%d" % l, [D, TL], BF16)
        xT_out = dout("xT_out", [D, TL])

        xT = es.enter_context(nc.sbuf_tensor("xT", [128, DC, TL], F32))
        hbuf = es.enter_context(nc.sbuf_tensor("hbuf", [128, DC, TL], BF16))
        epsc = es.enter_context(nc.sbuf_tensor("epsc", [128, 1], F32))
        onec = es.enter_context(nc.sbuf_tensor("onec", [128, 1], F32))
        qgs = es.enter_context(nc.sbuf_tensor("qgs", [128, DEPTH], F32))
        gcols = es.enter_context(nc.sbuf_tensor("gcols", [128, 2 * DEPTH, DC], F32))
        nrms = es.enter_context(nc.sbuf_tensor("nrms", [128, DEPTH, 4], F32))
        wgus = es.enter_context(nc.sbuf_tensor("wgus", [128, DEPTH, 128], F32))
        cbf = es.enter_context(nc.sbuf_tensor("cbf", [128, CO_TRIG], BF16))
        cf32 = es.enter_context(nc.sbuf_tensor("cf32", [128, NCONST - CO_TRIG], F32))
        ARENA = nc.sbuf_bytes_remaining - 64
        ARENA -= ARENA % 4
        arena = es.enter_context(nc.sbuf_tensor("arena", [128, ARENA // 4], F32))
        C.psum = [es.enter_context(nc.psum_tensor("ps%d" % i, [128, 512], F32)) for i in range(8)]
        C.xT, C.hbuf, C.epsc, C.onec, C.qgs = xT, hbuf, epsc, onec, qgs
        C.onesD = cbf[:, CO_ONESD:CO_ONESD + 128]
        C.onesH = cbf[:, CO_ONESH:CO_ONESH + 128]
        C.ones1 = cbf[:, CO_ONES1:CO_ONES1 + 128]
        C.triNeg = cbf[:, CO_TRINEG:CO_TRINEG + 128]
        C.maskd = cbf[:, CO_MASKD:CO_MASKD + 2048].rearrange("p (m t) -> p m t", t=512)
        C.ident = cbf[:, CO_IDENT:CO_IDENT + 128]
        C.triG = cf32[:, 0:128]
        C.triA = cf32[:, 128:256]
        C.maskG2 = cf32[:, 256:512]

        def many(off, n, shape, dt):
            out = []
            for _ in range(n):
                v, off = carve(arena, off, shape, dt)
                out.append(v)
            return out, off

        C.stg, off0 = many(0, 3, [128, 2048], F32)
        off = off0
        C.wb, off = many(off, 3, [128, 2048], BF16)
        C.upT, off = carve(arena, off, [128, 16, TL], BF16)
        C.relu, off = many(off, 2, [128, 512], F32)
        C.sqv, off = many(off, 2, [128, 512], BF16)
        C.lnv, off = carve(arena, off, [128, 512], F32)
        C.rstd, off = carve(arena, off, [128, 512], F32)
        assert off <= ARENA, (off, ARENA)
        off = 2 * 2048 * 4
        C.hT_t, off = carve(arena, off, [128, DC, 512], BF16)
        C.sqv2, off = many(off, 2, [128, 512], BF16)
        C.lnv2, off = carve(arena, off, [128, 512], F32)
        C.rstd2, off = carve(arena, off, [128, 512], F32)
        C.mixo, off = many(off, 2, [128, 2, 512], BF16)
        offm = off
        C.wsb, off = carve(arena, off, [128, 6, 2048], BF16)
        qn4, off = many(off, 4, [128, 512], BF16)
        C.qn2 = [qn4[0:2], qn4[2:4]]
        hT_b, off = carve(arena, off, [128, DC, 512], BF16)
        C.hT2 = [C.hT_t, hT_b]
        C.raw, off = many(off, 1, [128, 512], F32)
        C.E, off = many(off, 2, [128, 512], F32)
        C.Lp, off = many(off, 2, [128, 512], BF16)
        C.Wsb, off = many(off, 2, [128, 512], F32)
        C.A, off = many(off, 2, [128, 512], BF16)
        C.carry, off = carve(arena, off, [128, 512], F32)
        assert off <= ARENA, (off, ARENA)
        hflat = hbuf.rearrange("p a b -> p (a b)")
        C.KT = [hflat[:, h * SEQ:(h + 1) * SEQ] for h in range(2)]
        C.Vall = hflat[:, 2 * SEQ:4 * SEQ].rearrange("p (k d) -> p k d", d=256)
        off = offm
        C.wgl, off = carve(arena, off, [128, 7, 2048], BF16)
        C.gqT, off = carve(arena, off, [128, 512], F32)
        C.gkT, off = carve(arena, off, [128, 512], F32)
        C.sg, off = carve(arena, off, [128, 2, 512], F32)
        C.lrT, off = carve(arena, off, [128, 512], F32)
        C.gv_t, off = carve(arena, off, [128, 4, 256], BF16)
        C.gkt, off = carve(arena, off, [128, 4, 128], F32)
        C.en, off = many(off, 2, [128, 128], F32)
        C.spv, off = many(off, 2, [128, 128], F32)
        C.Eq, off = many(off, 2, [128, 128], F32)
        C.Ek, off = many(off, 2, [128, 128], F32)
        C.Ed, off = many(off, 2, [128, 128], F32)
        C.qe, off = many(off, 2, [128, 128], BF16)
        C.keh, off = many(off, 4, [128, 128], BF16)
        C.kdc, off = many(off, 4, [128, 128], BF16)
        C.scm, off = many(off, 2, [128, 256], BF16)
        C.Sf, off = many(off, 3, [128, 128], F32)
        C.Sbh, off = many(off, 6, [128, 128], BF16)
        C.osq2, off = carve(arena, off, [128, 256], BF16)
        C.lnv3, off = carve(arena, off, [128, 256], F32)
        C.rstd3, off = carve(arena, off, [128, 256], F32)
        C.otmp, off = carve(arena, off, [128, 256], F32)
        assert off <= ARENA, (off, ARENA)

        for dc in range(DC):
            S.add('sp', lambda e, dc=dc: e.dma_start(out=xT[:, dc, :], in_=xT_in[dc * 128:(dc + 1) * 128, :]),
                  writes=[('xT', dc, 0), ('xT', dc, 1)], dma=('x', dc % 4))
        S.add('sp', lambda e: e.dma_start(out=C.stg[0][:, 0:2048], in_=consts_d[:, 0:2048]), writes=['c0'], dma='c0')
        S.add('sp', lambda e: e.dma_start(out=C.stg[1][:, 0:CO_TRIG - 2048], in_=consts_d[:, 2048:CO_TRIG]), writes=['c1'], dma='c1')
        S.add('sp', lambda e: e.dma_start(out=cf32[:, :], in_=consts_d[:, CO_TRIG:NCONST]), writes=['c2'], dma='c2')
        S.add('dve', lambda e: e.tensor_copy(out=cbf[:, 0:2048], in_=C.stg[0][:, 0:2048]), reads=['c0'], writes=['cbf0'])
        S.add('dve', lambda e: e.tensor_copy(out=cbf[:, 2048:CO_TRIG], in_=C.stg[1][:, 0:CO_TRIG - 2048]), reads=['c1'], writes=['cbf1'])
        S.add('dve', lambda e: e.memset(epsc[:, :], EPS), writes=['epsc'])
        S.add('dve', lambda e: e.memset(onec[:, :], 1.0), writes=['onec'])
        for l in layers:
            if 'g1' in W[l]:
                S.add('sp', lambda e, l=l: e.dma_start(out=gcols[:, 2 * l, :], in_=W[l]['g1'][:, :]), writes=[('g', 2 * l)], dma=('g', 2 * l))
            if 'g2' in W[l]:
                S.add('sp', lambda e, l=l: e.dma_start(out=gcols[:, 2 * l + 1, :], in_=W[l]['g2'][:, :]), writes=[('g', 2 * l + 1)], dma=('g', 2 * l + 1))
            if 'nrm' in W[l]:
                S.add('sp', lambda e, l=l: e.dma_start(out=nrms[:, l, :], in_=W[l]['nrm'][:, :]), writes=[('nrm', l)], dma=('nrm', l))
                S.add('sp', lambda e, l=l: e.dma_start(out=wgus[0:17, l, :], in_=W[l]['wgu'][:, :]), writes=[('wgu', l)], dma=('wgu', l))
                S.add('dve', lambda e, l=l: e.tensor_scalar(out=qgs[:, l:l + 1], in0=nrms[:, l, 1:2], scalar1=float(128 ** -0.5), scalar2=None,
                                                           op0=ALU.mult), reads=[('nrm', l)], writes=[('qgs', l)])
        S.barrier()

        for (ph, l) in plan:
            if ph == 'N1':
                phase_norm(S, C, gcols[:, 2 * l, :], "n1")
                S.barrier()
                hloc = HX[('hloc', l)]
                if isinstance(hloc, list):
                    hall = HX[('hall', l)]
                    for q in range(4):
                        S.add('sp', lambda e, q=q, hloc=hloc: e.dma_start(out=hloc[q].rearrange("(c p) t -> p c t", p=128),
                                                                         in_=hbuf[:, :, q * 256:(q + 1) * 256]), writes=[('hloc', q)], dma=('hout', q))
                        S.add('pool', lambda e, q=q, hloc=hloc, hall=hall: e.collective_compute(
                            "AllGather", ALU.bypass, replica_groups=GROUPS, ins=[hloc[q].opt()], outs=[hall[q].opt()]),
                            reads=[('hloc', q)], writes=[('hall', q)], dma='cc', inc=1)
                    S.barrier(skip=('cc',), keep=lambda t: isinstance(t, tuple) and t[0] == 'hall')
                else:
                    S.add('sp', lambda e, hloc=hloc: e.dma_start(out=hloc.rearrange("(c p) t -> p c t", p=128), in_=hbuf[:, :, :]), dma='hout')
                    S.barrier()
            elif ph in ('MIX', 'SB', 'GLA'):
                mixloc = HX[('mixloc', l)]
                if ph in ('MIX', 'SB'):
                    phase_sb(S, C, W[l]['wsb'], HX[('hall', l)], mixloc, nrms[:, l, :], qgs[:, l:l + 1])
                    S.barrier()
                if ph in ('MIX', 'GLA'):
                    gather = None
                    if ('mixall', l) in HX:
                        mixall = HX[('mixall', l)]

                        def gather(q, mixloc=mixloc, mixall=mixall):
                            S.add('pool', lambda e: e.collective_compute("AllGather", ALU.bypass, replica_groups=GROUPS,
                                                                         ins=[mixloc[q].opt()], outs=[mixall[q].opt()]),
                                  reads=[('mixloc', q, 0, a, b) for a in range(2) for b in range(2)], writes=[('mixall', q)], dma='cc', inc=1)
                    phase_gla(S, C, W[l]['wgl'], HX[('hall', l)], mixloc, nrms[:, l, :], wgus[:, l, :], gather)
                    S.barrier(skip=('cc',), keep=lambda t: isinstance(t, tuple) and t[0] == 'mixall')
            elif ph == 'WO':
                if ('mixin', l) in HX:
                    mixin = HX[('mixin', l)]
                    S.add('sp', lambda e, mixin=mixin: e.dma_start(out=hbuf[:, :, :], in_=mixin.rearrange("(c p) t -> p c t", p=128)), dma='mixin')
                    S.barrier()
                else:
                    mixall = HX[('mixall', l)]

                    for q in range(4):
                        for src in range(4):
                            def fetch(e, q=q, src=src, mixall=mixall):
                                if 'rnk' not in C.cache:
                                    C.cache['rnk'] = e.snap(e.partition_id() % 4, min_val=0, max_val=3)
                                rnk = C.cache['rnk']
                                return e.dma_start(out=hbuf[:, src * 4:(src + 1) * 4, q * 256:(q + 1) * 256],
                                                   in_=mixall[q][bass.ds(rnk * 512 + src * D, 512), :].rearrange("(f p) t -> p f t", p=128))
                            S.add('sp', fetch, reads=[('mixall', q)], writes=[('hbufq', src, q)], dma=('mixin', q, src))
                phase_wo(S, C, W[l]['wo'])
                S.barrier()
            elif ph == 'MLP':
                phase_norm(S, C, gcols[:, 2 * l + 1, :], "n2")
                S.barrier()
                phase_mlp(S, C, W[l]['wup'], W[l]['wdn'])
                if (ph, l) != plan[-1]:
                    S.barrier()

        for dc in range(DC):
            S.add('sp', lambda e, dc=dc: e.dma_start(out=xT_out[dc * 128:(dc + 1) * 128, :], in_=xT[:, dc, :]),
                  reads=[('xT', dc, 0), ('xT', dc, 1)], dma=('xo', dc % 4))
        S.barrier()
        S.emit()
    return nc


def tile_w(w):
    K, M = w.shape
    t = w.reshape(K // 128, 128, M // 128, 128)
    t = t.transpose(2, 1, 0, 3)
    return np.ascontiguousarray(t.reshape(M // 128, 128, (K // 128) * 128))


def make_consts():
    c = np.zeros((128, NCONST), np.float32)
    c[:, CO_ONESD:CO_ONESD + 128] = 1.0 / D
    c[:, CO_ONESH:CO_ONESH + 128] = 1.0 / 128
    c[:, CO_ONES1:CO_ONES1 + 128] = 1.0
    p = np.arange(128)[:, None]
    q = np.arange(128)[None, :]
    c[:, CO_TRINEG:CO_TRINEG + 128] = np.where(p >= q, -1.0, 0.0)
    tc = np.arange(512)[None, :]
    for m in range(4):
        c[:, CO_MASKD + m * 512:CO_MASKD + (m + 1) * 512] = np.where(m * 128 + p < tc, 0.0, NEGM)
    c[:, CO_IDENT:CO_IDENT + 128] = np.where(p == q, 1.0, 0.0)
    same = (p // 64) == (q // 64)
    c[:, CO_TRIG:CO_TRIG + 128] = np.where((p <= q) & same, -1.0 / 16, 0.0)
    c[:, CO_TRIA:CO_TRIA + 128] = np.where((p > q) & same, -1.0 / 16, 0.0)
    mg = np.where((p <= q) & same, 1.0, 0.0)
    c[:, CO_MASKG2:CO_MASKG2 + 128] = mg
    c[:, CO_MASKG2 + 128:CO_MASKG2 + 256] = mg
    return c


def piece(wcols):
    return np.ascontiguousarray(wcols.reshape(DC, 128, 128).transpose(1, 0, 2).reshape(128, DC * 128))


def mixer_weights(w_in_l, w_gate_up_l, b_gate_l, gla_out_norm_l, sb_q_norm_l, sb_k_norm_l, sb_out_norm_l, j):
    GQ, GK, GV, GG, LR, SQ, SK, SV = 0, 512, 1024, 2048, 3072, 3088, 4112, 5136
    h0 = 2 * j
    sbp = []
    for base in (SQ, SK, SV):
        for h in (h0, h0 + 1):
            sbp.append(piece(w_in_l[:, base + h * 128:base + (h + 1) * 128]))
    wsb = np.stack(sbp)
    glp = [piece(w_in_l[:, GQ + h0 * 64:GQ + h0 * 64 + 128])]
    for h in (h0, h0 + 1):
        glp.append(piece(w_in_l[:, GG + h * 128:GG + (h + 1) * 128]))
    glp.append(piece(w_in_l[:, GK + h0 * 64:GK + h0 * 64 + 128]))
    for h in (h0, h0 + 1):
        glp.append(piece(w_in_l[:, GV + h * 128:GV + (h + 1) * 128]))
    lr = np.zeros((D, 128), np.float32)
    lr[:, 0:16] = w_in_l[:, LR:LR + 16]
    glp.append(piece(lr))
    wgl = np.stack(glp)
    nrm = np.ascontiguousarray(np.stack([gla_out_norm_l, sb_q_norm_l, sb_k_norm_l, sb_out_norm_l], axis=1)).astype(np.float32)
    wgu = np.zeros((17, 128), np.float32)
    wgu[0:16] = w_gate_up_l[:, h0 * 64:h0 * 64 + 128]
    wgu[16] = b_gate_l[h0 * 64:h0 * 64 + 128]
    return wsb, wgl, nrm, wgu


def wo_tiles(w_o_l):
    blocks = []
    for src in range(4):
        for f in range(4):
            row0 = (2 * src + f) * 128 if f < 2 else 1024 + (2 * src + f - 2) * 128
            blocks.append(w_o_l[row0:row0 + 128, :])
    return tile_w(np.concatenate(blocks, axis=0))


def gcol(g):
    return np.ascontiguousarray(g.reshape(DC, 128).T)


FUSED = True
_PROGRAMS = {}


def _program(plan):
    key = tuple(plan)
    if key not in _PROGRAMS:
        _PROGRAMS[key] = build_program(list(plan))
    return _PROGRAMS[key]


def _layer_inputs(l, j, P):
    wsb, wgl, nrm, wgu = mixer_weights(P["w_in"][l], P["w_gate_up"][l], P["b_gate"][l], P["gla_out_norm"][l],
                                       P["sb_q_norm"][l], P["sb_k_norm"][l], P["sb_out_norm"][l], j)
    return {"wsb_%d" % l: wsb, "wgl_%d" % l: wgl, "nrm_%d" % l: nrm, "wgu_%d" % l: wgu}


def kernel(x, attn_norm, w_in, w_gate_up, b_gate, gla_out_norm, sb_q_norm, sb_k_norm, sb_out_norm, w_o,
           mlp_norm, w_up, w_down):
    P = dict(w_in=np.asarray(w_in, np.float32), w_gate_up=np.asarray(w_gate_up, np.float32), b_gate=np.asarray(b_gate, np.float32),
             gla_out_norm=np.asarray(gla_out_norm, np.float32), sb_q_norm=np.asarray(sb_q_norm, np.float32),
             sb_k_norm=np.asarray(sb_k_norm, np.float32), sb_out_norm=np.asarray(sb_out_norm, np.float32))
    x = np.asarray(x, np.float32)
    attn_norm = np.asarray(attn_norm, np.float32)
    mlp_norm = np.asarray(mlp_norm, np.float32)
    w_o = np.asarray(w_o, np.float32)
    w_up = np.asarray(w_up, np.float32)
    w_down = np.asarray(w_down, np.float32)
    consts = make_consts()
    cores = list(range(NCORES))
    xT = [np.ascontiguousarray(x[c // 4].reshape(4, 4, 256, D)[:, c % 4].reshape(TL, D).T) for c in cores]
    shared = {}
    for l in range(DEPTH):
        shared[l] = {"g1_%d" % l: gcol(attn_norm[l]), "g2_%d" % l: gcol(mlp_norm[l]), "wo_%d" % l: wo_tiles(w_o[l]),
                     "wup_%d" % l: tile_w(w_up[l]),
                     "wdn_%d" % l: np.stack([tile_w(w_down[l][fg * 2048:(fg + 1) * 2048]) for fg in range(4)])}
    mixw = {(l, j): _layer_inputs(l, j, P) for l in range(DEPTH) for j in range(4)}

    def pick(d, names):
        return {k: d[k] for k in names}

    if FUSED:
        plan = []
        for l in range(DEPTH):
            plan += [('N1', l), ('MIX', l), ('WO', l), ('MLP', l)]
        nc = _program(plan)
        ins = []
        for c in cores:
            m = {"xT_in": xT[c], "consts": consts}
            for l in range(DEPTH):
                m.update(shared[l])
                m.update(mixw[(l, c % 4)])
            ins.append(m)
        res = run_bass_kernel_spmd(nc, ins, core_ids=cores)
        outT = [np.asarray(res.results[c]["xT_out"]) for c in cores]
    else:
        cur = xT
        h = None
        for l in range(DEPTH):
            if l == 0:
                nc = _program([('N1', 0)])
                ins = [dict(xT_in=cur[c], consts=consts, **pick(shared[0], ["g1_0"])) for c in cores]
                res = run_bass_kernel_spmd(nc, ins, core_ids=cores)
                h = [np.asarray(res.results[c]["h_out_0"]) for c in cores]
            hall = [np.concatenate([h[(c // 4) * 4 + r] for r in range(4)], axis=0) for c in cores]
            nc = _program([('MIX', l)])
            ins = [dict(xT_in=cur[c], consts=consts, **{"hall_%d" % l: hall[c]}, **mixw[(l, c % 4)]) for c in cores]
            res = run_bass_kernel_spmd(nc, ins, core_ids=cores)
            mo = [np.asarray(res.results[c]["mix_out_%d" % l]) for c in cores]
            mix_in = [np.concatenate([mo[(c // 4) * 4 + src][(c % 4) * 512:(c % 4 + 1) * 512] for src in range(4)], axis=0) for c in cores]
            plan = [('WO', l), ('MLP', l)] + ([('N1', l + 1)] if l + 1 < DEPTH else [])
            nc = _program(plan)
            names = ["g2_%d" % l, "wo_%d" % l, "wup_%d" % l, "wdn_%d" % l]
            ins = []
            for c in cores:
                m = dict(xT_in=cur[c], consts=consts, **{"mix_in_%d" % l: mix_in[c]}, **pick(shared[l], names))
                if l + 1 < DEPTH:
                    m["g1_%d" % (l + 1)] = shared[l + 1]["g1_%d" % (l + 1)]
                ins.append(m)
            res = run_bass_kernel_spmd(nc, ins, core_ids=cores)
            cur = [np.asarray(res.results[c]["xT_out"]) for c in cores]
            if l + 1 < DEPTH:
                h = [np.asarray(res.results[c]["h_out_%d" % (l + 1)]) for c in cores]
        outT = cur
    out = np.empty((2, SEQ, D), np.float32)
    for c in cores:
        out[c // 4].reshape(4, 4, 256, D)[:, c % 4] = outT[c].T.reshape(4, 256, D)
    return out
```

```python
import os
import numpy as np
import ml_dtypes
from contextlib import ExitStack
import concourse.bass as bass
import concourse.mybir as mybir
from concourse.bass_utils import run_bass_kernel_spmd

F32 = mybir.dt.float32
BF16 = mybir.dt.bfloat16
AF = mybir.ActivationFunctionType
ALU = mybir.AluOpType

NCORES = 8
D = 2048
DC = 16
TL = 1024
SEQ = 4096
DFF = 8192
EPS = 1e-6
DEPTH = 2


class Sched:
    def __init__(self, nc, es):
        self.nc = nc
        self.es = es
        self.engs = ['pe', 'act', 'dve', 'pool', 'sp']
        self.ops = {e: [] for e in self.engs}
        self.lastw = {}
        self.readers = {}
        self.dsem_cnt = {}
        self.dsems = {}
        self.esem = {e: es.enter_context(nc.semaphore("sem_" + e)) for e in self.engs}
        self.waited = {e: {} for e in self.engs}

    def _dep(self, eng, ev, waits):
        if ev[0] == 'e':
            if ev[1] == eng and eng in ('pe', 'sp'):
                return
            k = ('e', ev[1])
        else:
            k = ('d', ev[1])
        v = ev[2]
        if self.waited[eng].get(k, -1) >= v:
            return
        self.waited[eng][k] = v
        waits.append(ev)

    def add(self, eng, fn, reads=(), writes=(), dma=None, inc=16):
        cand = {}

        def c(ev):
            if ev is None:
                return
            k = (ev[0], ev[1])
            if k not in cand or cand[k][2] < ev[2]:
                cand[k] = ev
        for t in reads:
            c(self.lastw.get(t))
        for t in writes:
            c(self.lastw.get(t))
            for r in self.readers.get(t, ()):
                c(r)
        waits = []
        for ev in cand.values():
            self._dep(eng, ev, waits)
        idx = len(self.ops[eng])
        if dma is not None:
            if dma not in self.dsems:
                self.dsems[dma] = self.es.enter_context(self.nc.semaphore("d_%d" % len(self.dsems)))
                self.dsem_cnt[dma] = 0
            self.dsem_cnt[dma] += inc
            ev = ('d', dma, self.dsem_cnt[dma])
        else:
            ev = ('e', eng, idx)
        self.ops[eng].append(dict(fn=fn, waits=waits, dma=dma, inc=inc))
        for t in reads:
            self.readers.setdefault(t, []).append(ev)
        for t in writes:
            self.lastw[t] = ev
            self.readers[t] = []
        return ev

    def barrier(self, skip=(), keep=None):
        evs = []
        for e in self.engs:
            for i in range(len(self.ops[e]) - 1, -1, -1):
                op = self.ops[e][i]
                if op['dma'] is None and op['fn'] is not None:
                    evs.append(('e', e, i))
                    break
        for k, c in self.dsem_cnt.items():
            if k not in skip:
                evs.append(('d', k, c))
        for e in self.engs:
            waits = []
            for ev in evs:
                if ev[0] == 'e' and ev[1] == e:
                    continue
                self._dep(e, ev, waits)
            self.ops[e].append(dict(fn=None, waits=waits, dma=None, inc=16))
        kept = {t: w for t, w in self.lastw.items() if keep is not None and keep(t)}
        self.lastw.clear()
        self.readers.clear()
        self.lastw.update(kept)

    def emit(self):
        needed = {e: set() for e in self.engs}
        for e in self.engs:
            for op in self.ops[e]:
                for ev in op['waits']:
                    if ev[0] == 'e':
                        needed[ev[1]].add(ev[2])
        sig = {}
        for e in self.engs:
            c = 0
            for i in range(len(self.ops[e])):
                if i in needed[e]:
                    c += 1
                    sig[(e, i)] = c
        S = self

        def run(ename, eng):
            for i, op in enumerate(S.ops[ename]):
                for ev in op['waits']:
                    if ev[0] == 'e':
                        eng.wait_ge(S.esem[ev[1]], sig[(ev[1], ev[2])])
                    else:
                        eng.wait_ge(S.dsems[ev[1]], ev[2])
                if op['fn'] is None:
                    continue
                inst = op['fn'](eng)
                if op['dma'] is not None:
                    if op['inc'] == 1:
                        inst.then_inc(S.dsems[op['dma']])
                    else:
                        inst.then_inc(S.dsems[op['dma']], op['inc'])
                elif (ename, i) in sig:
                    inst.then_inc(S.esem[ename], 1)

        with self.nc.Block() as block:
            @block.tensor
            def _(pe):
                run('pe', pe)

            @block.scalar
            def _(act):
                run('act', act)

            @block.vector
            def _(dve):
                run('dve', dve)

            @block.gpsimd
            def _(pool):
                run('pool', pool)

            @block.sync
            def _(sp):
                run('sp', sp)


class Ctx:
    pass


def carve(arena, off, shape, dt):
    n = 1
    for s in shape[1:]:
        n *= s
    esz = 4 if dt == F32 else 2
    nbytes = n * esz
    assert off % 4 == 0 and nbytes % 4 == 0
    v = arena[0:shape[0], off // 4:(off + nbytes) // 4]
    if dt != F32:
        v = v.bitcast(dt)
    if len(shape) == 3:
        v = v.rearrange("p (a b) -> p a b", b=shape[2])
    elif len(shape) == 4:
        v = v.rearrange("p (a b c) -> p a b c", b=shape[2], c=shape[3])
    return v, off + nbytes


def phase_norm(S, C, gcol, tag):
    xT, hbuf = C.xT, C.hbuf
    for half in range(2):
        sl = slice(half * 512, (half + 1) * 512)
        ps = C.psum[0]
        for dc in range(DC):
            sq = C.sqv[dc % 2]
            S.add('act', lambda e, dc=dc, sq=sq, sl=sl: e.activation(out=sq[:, :], in_=xT[:, dc, sl], func=AF.Square),
                  reads=[('xT', dc, half)], writes=[('sqv', dc % 2)])
            S.add('pe', lambda e, dc=dc, sq=sq, ps=ps: e.matmul(ps[:, :], lhsT=C.onesD[:, :], rhs=sq[:, :],
                                                               start=(dc == 0), stop=(dc == DC - 1)),
                  reads=[('sqv', dc % 2)], writes=['ps0'])
        S.add('act', lambda e, ps=ps: e.activation(out=C.lnv[:, :], in_=ps[:, :], func=AF.Ln, bias=C.epsc[:, 0:1], scale=1.0),
              reads=['ps0'], writes=['lnv'])
        S.add('act', lambda e: e.activation(out=C.rstd[:, :], in_=C.lnv[:, :], func=AF.Exp, scale=-0.5),
              reads=['lnv'], writes=['rstd'])
        for dc in range(DC):
            S.add('dve', lambda e, dc=dc, sl=sl: e.scalar_tensor_tensor(
                out=hbuf[:, dc, sl], in0=xT[:, dc, sl], scalar=gcol[:, dc:dc + 1], in1=C.rstd[:, :],
                op0=ALU.mult, op1=ALU.mult),
                reads=[('xT', dc, half), 'rstd'], writes=[('hbuf', dc, half)])


def load_wtile(S, C, w_ap, k):
    ss = k % len(C.stg)
    sb = k % 3
    stg = C.stg[ss]
    wb = C.wb[sb]
    S.add('sp', lambda e: e.dma_start(out=stg[:, :], in_=w_ap), writes=[('stg', ss)], dma=('stg', ss))
    if k % 2 == 0:
        S.add('act', lambda e: e.activation(out=wb[:, :], in_=stg[:, :], func=AF.Copy), reads=[('stg', ss)], writes=[('wb', sb)])
    else:
        S.add('dve', lambda e: e.tensor_copy(out=wb[:, :], in_=stg[:, :]), reads=[('stg', ss)], writes=[('wb', sb)])
    return wb.rearrange("p (a b) -> p a b", b=128), ('wb', sb)


def phase_wo(S, C, wo_d):
    for oc in range(DC):
        wt, wtok = load_wtile(S, C, wo_d[oc, :, :], C.wk)
        C.wk += 1
        for half in range(2):
            sl = slice(half * 512, (half + 1) * 512)
            b = C.pk % 4
            C.pk += 1
            ps = C.psum[b]
            for ic in range(DC):
                S.add('pe', lambda e, ic=ic, ps=ps, wt=wt, sl=sl: e.matmul(ps[:, :], lhsT=wt[:, ic, :], rhs=C.hbuf[:, ic, sl],
                                                                         start=(ic == 0), stop=(ic == DC - 1)),
                      reads=[wtok, ('hbufq', ic // 4, 2 * half), ('hbufq', ic // 4, 2 * half + 1)], writes=[('ps', b)])
            S.add('dve', lambda e, oc=oc, ps=ps, sl=sl: e.tensor_tensor(out=C.xT[:, oc, sl], in0=C.xT[:, oc, sl], in1=ps[:, :], op=ALU.add),
                  reads=[('ps', b)], writes=[('xT', oc, half)])


def phase_mlp(S, C, wup_d, wdn_d):
    for fg in range(4):
        for fc in range(16):
            wt, wtok = load_wtile(S, C, wup_d[fg * 16 + fc, :, :], C.wk)
            C.wk += 1
            for half in range(2):
                sl = slice(half * 512, (half + 1) * 512)
                b = C.pk % 4
                C.pk += 1
                ps = C.psum[b]
                for dc in range(DC):
                    S.add('pe', lambda e, dc=dc, ps=ps, wt=wt, sl=sl: e.matmul(ps[:, :], lhsT=wt[:, dc, :], rhs=C.hbuf[:, dc, sl],
                                                                             start=(dc == 0), stop=(dc == DC - 1)),
                          reads=[wtok, ('hbuf', dc, half)], writes=[('ps', b)])
                r = C.rk % 2
                C.rk += 1
                rt = C.relu[r]
                S.add('act', lambda e, ps=ps, rt=rt: e.activation(out=rt[:, :], in_=ps[:, :], func=AF.Relu),
                      reads=[('ps', b)], writes=[('relu', r)])
                S.add('pool', lambda e, rt=rt, fc=fc, sl=sl: e.tensor_tensor(out=C.upT[:, fc, sl], in0=rt[:, :], in1=rt[:, :], op=ALU.mult),
                      reads=[('relu', r)], writes=[('upT', fc, half)])
        for dc in range(DC):
            wt, wtok = load_wtile(S, C, wdn_d[fg, dc, :, :], C.wk)
            C.wk += 1
            for half in range(2):
                sl = slice(half * 512, (half + 1) * 512)
                b = 4 + C.pk2 % 4
                C.pk2 += 1
                ps = C.psum[b]
                for fc in range(16):
                    S.add('pe', lambda e, fc=fc, ps=ps, wt=wt, sl=sl: e.matmul(ps[:, :], lhsT=wt[:, fc, :], rhs=C.upT[:, fc, sl],
                                                                             start=(fc == 0), stop=(fc == 15)),
                          reads=[wtok, ('upT', fc, half)], writes=[('ps', b)])
                S.add('dve', lambda e, dc=dc, ps=ps, sl=sl: e.tensor_tensor(out=C.xT[:, dc, sl], in0=C.xT[:, dc, sl], in1=ps[:, :], op=ALU.add),
                      reads=[('ps', b)], writes=[('xT', dc, half)])


CO_ONESD, CO_ONESH, CO_ONES1, CO_TRINEG = 0, 128, 256, 384
CO_MASKD = 512
CO_IDENT = 2560
CO_TRIG = 2688
CO_TRIA = 2816
CO_MASKG2 = 2944
NCONST = 3200
NEGM = -30000.0


def load_pieces(S, C, w_d, npieces, dst):
    for k in range(npieces):
        ss = k % 2
        stg = C.stg[ss]
        S.add('sp', lambda e, k=k, stg=stg: e.dma_start(out=stg[:, :], in_=w_d[k, :, :]), writes=[('stg', ss)], dma=('stg', ss))
        if k % 2 == 0:
            S.add('act', lambda e, k=k, stg=stg: e.activation(out=dst[:, k, :], in_=stg[:, :], func=AF.Copy), reads=[('stg', ss)], writes=[('wpc', k)])
        else:
            S.add('dve', lambda e, k=k, stg=stg: e.tensor_copy(out=dst[:, k, :], in_=stg[:, :]), reads=[('stg', ss)], writes=[('wpc', k)])


def head_norm(S, C, ps_ap, ps_tok, n, gain_ap, dst_ap, dst_tok, ms_ap, ms_tok, raw=None, raw_tok=None):
    k = C.nk % 2
    C.nk += 1
    sq = C.sqv2[k][:, 0:n]
    lnv = C.lnv2[:, 0:n]
    rstd = C.rstd2[:, 0:n]
    src = ps_ap
    src_tok = ps_tok
    if raw is not None:
        S.add('act', lambda e: e.activation(out=raw, in_=ps_ap, func=AF.Copy), reads=[ps_tok], writes=[raw_tok])
        src, src_tok = raw, raw_tok
    S.add('act', lambda e: e.activation(out=sq, in_=ps_ap, func=AF.Square), reads=[ps_tok], writes=[('sqv2', k)])
    S.add('pe', lambda e: e.matmul(ms_ap, lhsT=C.onesH[:, :], rhs=sq, start=True, stop=True), reads=[('sqv2', k)], writes=[ms_tok])
    S.add('act', lambda e: e.activation(out=lnv, in_=ms_ap, func=AF.Ln, bias=C.epsc[:, 0:1], scale=1.0), reads=[ms_tok], writes=['lnv2'])
    S.add('act', lambda e: e.activation(out=rstd, in_=lnv, func=AF.Exp, scale=-0.5), reads=['lnv2'], writes=['rstd2'])
    S.add('dve', lambda e: e.scalar_tensor_tensor(out=dst_ap, in0=src, scalar=gain_ap, in1=rstd, op0=ALU.mult, op1=ALU.mult),
          reads=[src_tok, 'rstd2'], writes=[dst_tok])


def load_hT(S, C, hall, i):
    q = i // 2
    for hq in range(2):
        r = (i % 2) * 2 + hq
        if isinstance(hall, list):
            src = hall[q][r * D:(r + 1) * D, :]
        else:
            src = hall[r * D:(r + 1) * D, q * 256:(q + 1) * 256]
        S.add('sp', lambda e, hq=hq, src=src: e.dma_start(out=C.hT_t[:, :, hq * 256:(hq + 1) * 256], in_=src.rearrange("(c p) t -> p c t", p=128)),
              reads=[('hall', q)], writes=['hT_t%d' % hq], dma=('hT', hq))


def store_mix(S, C, mixloc, i, mb, row0, reads):
    q = i // 2
    for hq in range(2):
        r = (i % 2) * 2 + hq
        if isinstance(mixloc, list):
            dst = mixloc[q][r * 512 + row0:r * 512 + row0 + 256, :]
        else:
            dst = mixloc[r * 512 + row0:r * 512 + row0 + 256, q * 256:(q + 1) * 256]
        S.add('sp', lambda e, hq=hq, dst=dst: e.dma_start(out=dst.rearrange("(h p) t -> p h t", p=128), in_=C.mixo[mb][:, :, hq * 256:(hq + 1) * 256]),
              reads=reads, writes=[('mixloc', q, row0, i % 2, hq)], dma=('mixo', mb, hq))


def phase_sb(S, C, wsb_d, hall, mixloc, nrm, qgs, preloaded=False):
    ps = C.psum
    wv = C.wsb.rearrange("p k (c m) -> p k c m", m=128)
    if not preloaded:
        load_pieces(S, C, wsb_d, 6, C.wsb)
    NT = 8

    def proj_chunks(i):
        bf = i % 2
        hT = C.hT2[bf]
        htoks = [('hT_t', bf, 0), ('hT_t', bf, 1)]
        t0 = i * 512
        out = []
        for pi in range(4):
            def mm(pi=pi):
                for dc in range(DC):
                    S.add('pe', lambda e, dc=dc: e.matmul(ps[6][:, :], lhsT=wv[:, pi, dc, :], rhs=hT[:, dc, :], start=(dc == 0), stop=(dc == DC - 1)),
                          reads=[('wpc', pi)] + htoks, writes=[('ps', 6)])

            def nrmf(pi=pi):
                h = pi % 2
                rk = 0
                if pi < 2:
                    head_norm(S, C, ps[6][:, :], ('ps', 6), 512, qgs, C.qn2[bf][h][:, :], ('qn', bf, h), ps[7][:, :], ('ps', 7),
                              raw=C.raw[rk][:, :], raw_tok=('raw', rk))
                else:
                    head_norm(S, C, ps[6][:, :], ('ps', 6), 512, nrm[:, 2:3], C.KT[h][:, t0:t0 + 512], ('KT', h, i), ps[7][:, :], ('ps', 7),
                              raw=C.raw[rk][:, :], raw_tok=('raw', rk))
            out += [mm, nrmf]
        for s in range(4):
            def vmm(s=s):
                for dc in range(DC):
                    S.add('pe', lambda e, dc=dc: e.matmul(ps[6][:, 0:256], lhsT=hT[:, dc, s * 128:(s + 1) * 128],
                                                         rhs=wv[:, 4:6, dc, :], start=(dc == 0), stop=(dc == DC - 1)),
                          reads=[('wpc', 4), ('wpc', 5)] + htoks, writes=[('ps', 6)])
                blk = i * 4 + s
                S.add('dve', lambda e: e.tensor_copy(out=C.Vall[:, blk, :], in_=ps[6][:, 0:256]), reads=[('ps', 6)], writes=[('V', blk)])
            out.append(vmm)
        return out

    def load(i):
        q = i // 2
        for hq in range(2):
            r = (i % 2) * 2 + hq
            src = hall[q][r * D:(r + 1) * D, :] if isinstance(hall, list) else hall[r * D:(r + 1) * D, q * 256:(q + 1) * 256]
            S.add('sp', lambda e, hq=hq, src=src: e.dma_start(out=C.hT2[i % 2][:, :, hq * 256:(hq + 1) * 256], in_=src.rearrange("(c p) t -> p c t", p=128)),
                  reads=[('hall', q)], writes=[('hT_t', i % 2, hq)], dma=('hT', i % 2, hq))

    load(0)
    for ch in proj_chunks(0):
        ch()
    for i in range(NT):
        bf = i % 2
        qn = C.qn2[bf]
        pending = []
        if i + 1 < NT:
            load(i + 1)
            pending = proj_chunks(i + 1)
        steps = []
        nkb = 4 * i + 4
        for h in range(2):
            for kb in range(nkb - 1, -1, -1):
                steps.append((h, kb, kb == nkb - 1, kb == 0, kb - 4 * i))
        n = len(steps)
        mb = i % 2
        per = -(-len(pending) // n) if pending else 0

        def st_z(t):
            h, kb, first, last, m = steps[t]
            k = t % 2
            qh = qn[h]
            S.add('pe', lambda e: e.matmul(ps[k][:, :], lhsT=C.KT[h][:, kb * 128:(kb + 1) * 128], rhs=qh[:, :], start=True, stop=(m < 0)),
                  reads=[('KT', h, kb // 4), ('qn', bf, h)], writes=[('ps', k)])
            if m >= 0:
                S.add('pe', lambda e: e.matmul(ps[k][:, :], lhsT=C.ident[:, :], rhs=C.maskd[:, m, :], start=False, stop=True), writes=[('ps', k)])
            S.add('act', lambda e: e.activation(out=C.E[k][:, :], in_=ps[k][:, :], func=AF.Exp), reads=[('ps', k)], writes=[('E', k)])
            S.add('act', lambda e: e.activation(out=C.Lp[k][:, :], in_=C.E[k][:, :], func=AF.Ln, bias=C.onec[:, 0:1], scale=1.0),
                  reads=[('E', k)], writes=[('Lp', k)])

        def st_w(t):
            h, kb, first, last, m = steps[t]
            k = t % 2
            qh = qn[h]
            S.add('pe', lambda e: e.matmul(ps[2 + k][:, :], lhsT=C.KT[h][:, kb * 128:(kb + 1) * 128], rhs=qh[:, :], start=True, stop=False),
                  reads=[('KT', h, kb // 4), ('qn', bf, h)], writes=[('ps', 2 + k)])
            if m >= 0:
                S.add('pe', lambda e: e.matmul(ps[2 + k][:, :], lhsT=C.ident[:, :], rhs=C.maskd[:, m, :], start=False, stop=False), writes=[('ps', 2 + k)])
            S.add('pe', lambda e: e.matmul(ps[2 + k][:, :], lhsT=C.triNeg[:, :], rhs=C.Lp[k][:, :], start=False, stop=True),
                  reads=[('Lp', k)], writes=[('ps', 2 + k)])
            S.add('pe', lambda e: e.matmul(ps[4][:, :], lhsT=C.ones1[:, :], rhs=C.Lp[k][:, :], start=True, stop=True),
                  reads=[('Lp', k)], writes=[('ps', 4)])
            if first:
                S.add('dve', lambda e: e.memset(C.carry[:, :], 0.0), writes=['carry'])
            S.add('dve', lambda e: e.tensor_tensor(out=C.Wsb[k][:, :], in0=ps[2 + k][:, :], in1=C.carry[:, :], op=ALU.subtract),
                  reads=[('ps', 2 + k), 'carry'], writes=[('Wsb', k)])
            S.add('dve', lambda e: e.tensor_tensor(out=C.carry[:, :], in0=C.carry[:, :], in1=ps[4][:, :], op=ALU.add),
                  reads=['carry', ('ps', 4)], writes=['carry'])

        def st_a(t):
            k = t % 2
            S.add('act', lambda e: e.activation(out=C.A[k][:, :], in_=C.Wsb[k][:, :], func=AF.Exp), reads=[('Wsb', k)], writes=[('A', k)])

        def st_av(t):
            h, kb, first, last, m = steps[t]
            k = t % 2
            S.add('pe', lambda e: e.matmul(ps[5][:, :], lhsT=C.Vall[:, kb, h * 128:(h + 1) * 128], rhs=C.A[k][:, :], start=first, stop=last),
                  reads=[('V', kb), ('A', k)], writes=[('ps', 5)])
            if last:
                head_norm(S, C, ps[5][:, :], ('ps', 5), 512, nrm[:, 3:4], C.mixo[mb][:, h, :], ('mixo', mb, h), ps[7][:, :], ('ps', 7))

        for t in range(n + 3):
            if t < n:
                st_z(t)
            if 1 <= t <= n:
                st_w(t - 1)
            if 2 <= t <= n + 1:
                st_a(t - 2)
            if t >= 3:
                st_av(t - 3)
            for _ in range(per):
                if pending:
                    pending.pop(0)()
        while pending:
            pending.pop(0)()
        store_mix(S, C, mixloc, i, mb, 256, [('mixo', mb, 0), ('mixo', mb, 1)])


def phase_gla(S, C, wgl_d, hall, mixloc, nrm, wgu, gather=None):
    ps = C.psum
    wv = C.wgl.rearrange("p k (c m) -> p k c m", m=128)
    load_pieces(S, C, wgl_d, 7, C.wgl)
    S.add('dve', lambda e: e.memset(C.lrT[:, :], 1.0), writes=['lrT'])
    S.add('dve', lambda e: e.memset(C.Sf[0][:, :], 0.0), writes=[('Sf', 0)])
    for q in range(6):
        S.add('pool', lambda e, q=q: e.memset(C.Sbh[q][:, :], 0.0), writes=[('Sb', q // 2)])
    for q in range(4):
        S.add('pool', lambda e, q=q: e.memset(C.keh[q][:, :], 0.0), writes=[('ke', q // 2)])
        S.add('pool', lambda e, q=q: e.memset(C.kdc[q][:, :], 0.0), writes=[('kd', q // 2)])
    pj = 0
    sk = 0
    for i in range(int(os.environ.get('G_TILES', '8'))):
        r, c0 = i // 2, (i % 2) * 512
        load_hT(S, C, hall, i)
        mb = i % 2
        for pi, kind in ((0, 'gq'), (3, 'gk'), (1, 'gg0'), (2, 'gg1'), (6, 'lr')):
            b = pj % 2
            pj += 1
            M = 16 if kind == 'lr' else 128
            for dc in range(DC):
                S.add('pe', lambda e, pi=pi, dc=dc, b=b, M=M: e.matmul(ps[b][0:M, :], lhsT=wv[:, pi, dc, 0:M], rhs=C.hT_t[:, dc, :],
                                                                   start=(dc == 0), stop=(dc == DC - 1)),
                      reads=[('wpc', pi), 'hT_t0', 'hT_t1'], writes=[('ps', b)])
            if kind == 'gq':
                S.add('act', lambda e, b=b: e.activation(out=C.gqT[:, :], in_=ps[b][:, :], func=AF.Copy), reads=[('ps', b)], writes=['gqT'])
            elif kind == 'gk':
                S.add('act', lambda e, b=b: e.activation(out=C.gkT[:, :], in_=ps[b][:, :], func=AF.Copy), reads=[('ps', b)], writes=['gkT'])
            elif kind in ('gg0', 'gg1'):
                hh = 0 if kind == 'gg0' else 1
                S.add('act', lambda e, b=b, hh=hh: e.activation(out=C.sg[:, hh, :], in_=ps[b][:, :], func=AF.Silu), reads=[('ps', b)], writes=[('sg', hh)])
            else:
                S.add('act', lambda e, b=b: e.activation(out=C.lrT[0:16, :], in_=ps[b][0:16, :], func=AF.Copy), reads=[('ps', b)], writes=['lrT'])
        GST = int(os.environ.get('G_STAGE', '9'))
        for s in range(4 if GST >= 2 else 0):
            for dc in range(DC):
                S.add('pe', lambda e, dc=dc, s=s: e.matmul(ps[2][:, 0:384], lhsT=C.hT_t[:, dc, s * 128:(s + 1) * 128],
                                                          rhs=wv[:, 3:6, dc, :], start=(dc == 0), stop=(dc == DC - 1)),
                      reads=[('wpc', 3), ('wpc', 4), ('wpc', 5), 'hT_t0', 'hT_t1'], writes=[('ps', 2)])
            S.add('dve', lambda e, s=s: e.tensor_copy(out=C.gv_t[:, s, :], in_=ps[2][:, 128:384]), reads=[('ps', 2)], writes=[('gv_t', s)])
            S.add('dve', lambda e, s=s: e.tensor_copy(out=C.gkt[:, s, :], in_=ps[2][:, 0:128]), reads=[('ps', 2)], writes=[('gkt', s)])
        for s in range(4 if GST >= 3 else 0):
            ts = slice(s * 128, (s + 1) * 128)
            k = s % 2
            en, spv, Eq, Ek, Ed = C.en[k], C.spv[k], C.Eq[k], C.Ek[k], C.Ed[k]
            qe, scm = C.qe[k], C.scm[k]
            kdc = (C.kdc[2 * k], C.kdc[2 * k + 1])
            keh = (C.keh[2 * k], C.keh[2 * k + 1])
            S.add('pe', lambda e, ts=ts: e.matmul(ps[3][:, 0:128], lhsT=C.lrT[0:17, ts], rhs=wgu[0:17, :], start=True, stop=True),
                  reads=['lrT'], writes=[('ps', 3)])
            S.add('act', lambda e, ts=ts, en=en: e.activation(out=en[:, :], in_=ps[3][:, 0:128], func=AF.Exp, scale=-1.0), reads=[('ps', 3)], writes=[('en', k)])
            S.add('act', lambda e, en=en, spv=spv: e.activation(out=spv[:, :], in_=en[:, :], func=AF.Ln, bias=C.onec[:, 0:1], scale=1.0),
                  reads=[('en', k)], writes=[('spv', k)])
            S.add('pe', lambda e, ts=ts, spv=spv: e.matmul(ps[4][:, 0:128], lhsT=spv[:, :], rhs=C.triG[:, :], start=True, stop=True),
                  reads=[('spv', k)], writes=[('ps', 4)])
            S.add('pe', lambda e, ts=ts, spv=spv: e.matmul(ps[4][:, 128:256], lhsT=C.triA[:, :], rhs=spv[:, :], start=True, stop=True),
                  reads=[('spv', k)], writes=[('ps', 4)])
            S.add('act', lambda e, ts=ts, Eq=Eq: e.activation(out=Eq[:, :], in_=ps[4][:, 0:128], func=AF.Exp), reads=[('ps', 4)], writes=[('Eq', k)])
            S.add('act', lambda e, ts=ts, Ek=Ek: e.activation(out=Ek[:, :], in_=ps[4][:, 0:128], func=AF.Exp, scale=-1.0), reads=[('ps', 4)], writes=[('Ek', k)])
            S.add('act', lambda e, ts=ts, Ed=Ed: e.activation(out=Ed[:, :], in_=ps[4][:, 128:256], func=AF.Exp), reads=[('ps', 4)], writes=[('Ed', k)])
            S.add('dve', lambda e, ts=ts, qe=qe, Eq=Eq: e.scalar_tensor_tensor(out=qe[:, :], in0=C.gqT[:, ts], scalar=0.125, in1=Eq[:, :],
                                                                             op0=ALU.mult, op1=ALU.mult),
                  reads=['gqT', ('Eq', k)], writes=[('qe', k)])
            for hh in range(2):
                S.add('dve', lambda e, ts=ts, hh=hh, keh=keh, Ek=Ek: e.tensor_tensor(out=keh[hh][64 * hh:64 * hh + 64, :], in0=C.gkT[64 * hh:64 * hh + 64, ts],
                                                                                in1=Ek[64 * hh:64 * hh + 64, :], op=ALU.mult),
                      reads=['gkT', ('Ek', k)], writes=[('ke', k)])
            for cc in range(2):
                S.add('dve', lambda e, s=s, cc=cc, kdc=kdc, Ed=Ed: e.tensor_tensor(out=kdc[cc][64 * cc:64 * cc + 64, :], in0=C.gkt[64 * cc:64 * cc + 64, s, :],
                                                                              in1=Ed[64 * cc:64 * cc + 64, :], op=ALU.mult),
                      reads=[('gkt', s), ('Ed', k)], writes=[('kd', k)])
            if GST < 4:
                continue
            for hh in range(2):
                S.add('pe', lambda e, hh=hh, keh=keh, qe=qe: e.matmul(ps[5][:, hh * 128:(hh + 1) * 128], lhsT=keh[hh][:, :],
                                                                     rhs=qe[:, :], start=True, stop=True),
                      reads=[('ke', k), ('qe', k)], writes=[('ps', 5)])
            S.add('dve', lambda e, scm=scm: e.tensor_tensor(out=scm[:, :], in0=ps[5][:, 0:256], in1=C.maskG2[:, :], op=ALU.mult),
                  reads=[('ps', 5)], writes=[('scm', k)])
            GSUB = int(os.environ.get('G_SUB', '9'))
            if GSUB < 2:
                continue
            for cc in range(2):
                for hh in range(2):
                    S.add('pe', lambda e, cc=cc, hh=hh, kdc=kdc, s=s: e.matmul(
                        ps[6][:, (cc * 2 + hh) * 128:(cc * 2 + hh + 1) * 128], lhsT=kdc[cc][:, :],
                        rhs=C.gv_t[:, s, hh * 128:(hh + 1) * 128], start=True, stop=True),
                        reads=[('kd', k), ('gv_t', s)], writes=[('ps', 6)])
            s0, s1, s2 = sk % 3, (sk + 1) % 3, (sk + 2) % 3
            sk += 2
            if GSUB < 3:
                continue
            for cc, sa, sb_ in ((0, s0, s1), (1, s1, s2)):
                for hh in range(2):
                    pr = slice(64 * hh, 64 * hh + 64)
                    S.add('dve', lambda e, cc=cc, sa=sa, sb_=sb_, hh=hh, pr=pr, Eq=Eq: e.scalar_tensor_tensor(
                        out=C.Sf[sb_][pr, :], in0=C.Sf[sa][pr, :], scalar=Eq[pr, cc * 64 + 63:cc * 64 + 64],
                        in1=ps[6][pr, (cc * 2 + hh) * 128:(cc * 2 + hh + 1) * 128], op0=ALU.mult, op1=ALU.add),
                        reads=[('Sf', sa), ('Eq', k), ('ps', 6)], writes=[('Sf', sb_)])
                for hh in range(2):
                    S.add('act', lambda e, sb_=sb_, hh=hh: e.activation(out=C.Sbh[2 * sb_ + hh][64 * hh:64 * hh + 64, :], in_=C.Sf[sb_][64 * hh:64 * hh + 64, :], func=AF.Copy),
                          reads=[('Sf', sb_)], writes=[('Sb', sb_)])
            if GST < 5:
                continue
            for hh in range(2):
                S.add('pe', lambda e, hh=hh, s=s, scm=scm: e.matmul(ps[7][:, hh * 128:(hh + 1) * 128], lhsT=C.gv_t[:, s, hh * 128:(hh + 1) * 128],
                                                                    rhs=scm[:, hh * 128:(hh + 1) * 128], start=True, stop=False),
                      reads=[('gv_t', s), ('scm', k)], writes=[('ps', 7)])
                for cc, sx in ((0, s0), (1, s1)):
                    S.add('pe', lambda e, hh=hh, cc=cc, sx=sx, qe=qe: e.matmul(
                        ps[7][:, hh * 128 + cc * 64:hh * 128 + (cc + 1) * 64], lhsT=C.Sbh[2 * sx + hh][:, :],
                        rhs=qe[:, cc * 64:(cc + 1) * 64], start=False, stop=(cc == 1)),
                        reads=[('Sb', sx), ('qe', k)], writes=[('ps', 7)])
            S.add('act', lambda e: e.activation(out=C.osq2[:, :], in_=ps[7][:, 0:256], func=AF.Square), reads=[('ps', 7)], writes=['osq2'])
            S.add('pe', lambda e: e.matmul(ps[5][:, 256:512], lhsT=C.onesH[:, :], rhs=C.osq2[:, :], start=True, stop=True),
                  reads=['osq2'], writes=[('ps', 5)])
            S.add('act', lambda e: e.activation(out=C.lnv3[:, :], in_=ps[5][:, 256:512], func=AF.Ln, bias=C.epsc[:, 0:1], scale=1.0),
                  reads=[('ps', 5)], writes=['lnv3'])
            S.add('act', lambda e: e.activation(out=C.rstd3[:, :], in_=C.lnv3[:, :], func=AF.Exp, scale=-0.5), reads=['lnv3'], writes=['rstd3'])
            S.add('dve', lambda e: e.scalar_tensor_tensor(out=C.otmp[:, :], in0=ps[7][:, 0:256], scalar=nrm[:, 0:1], in1=C.rstd3[:, :],
                                                          op0=ALU.mult, op1=ALU.mult),
                  reads=[('ps', 7), 'rstd3'], writes=['otmp'])
            if GST < 6:
                continue
            S.add('pool', lambda e, ts=ts, mb=mb: e.tensor_tensor(out=C.mixo[mb][:, :, ts], in0=C.otmp.rearrange("p (h c) -> p h c", h=2),
                                                                in1=C.sg[:, :, ts], op=ALU.mult),
                  reads=['otmp', ('sg', 0), ('sg', 1)], writes=[('mixo', mb, s)])
        store_mix(S, C, mixloc, i, mb, 0, [('mixo', mb, s) for s in range(4)])
        if gather is not None and i % 2 == 1:
            gather(i // 2)


GROUPS = [[0, 1, 2, 3], [4, 5, 6, 7]]


def build_program(plan):
    nc = bass.Bass("TRN2", target_bir_lowering=False)
    es = ExitStack()
    with es:
        S = Sched(nc, es)
        C = Ctx()
        C.wk = C.pk = C.pk2 = C.rk = C.nk = C.rawk = C.ak = 0
        C.cache = {}
        layers = sorted(set(l for _, l in plan))

        def din(name, shape, dt=F32):
            return nc.dram_tensor(name, shape, dt, kind="ExternalInput").ap()

        def dout(name, shape, dt=F32):
            return nc.dram_tensor(name, shape, dt, kind="ExternalOutput").ap()

        def dint(name, shape, dt):
            return nc.dram_tensor(name, shape, dt).ap()

        xT_in = din("xT_in", [D, TL])
        consts_d = din("consts", [128, NCONST])
        W = {}
        for l in layers:
            W[l] = {}
            if ('N1', l) in plan:
                W[l]['g1'] = din("g1_%d" % l, [128, DC])
            if ('MIX', l) in plan or ('SB', l) in plan or ('GLA', l) in plan:
                W[l]['wsb'] = din("wsb_%d" % l, [6, 128, 2048])
                W[l]['wgl'] = din("wgl_%d" % l, [7, 128, 2048])
                W[l]['nrm'] = din("nrm_%d" % l, [128, 4])
                W[l]['wgu'] = din("wgu_%d" % l, [17, 128])
            if ('WO', l) in plan:
                W[l]['wo'] = din("wo_%d" % l, [DC, 128, 2048])
            if ('MLP', l) in plan:
                W[l]['g2'] = din("g2_%d" % l, [128, DC])
                W[l]['wup'] = din("wup_%d" % l, [64, 128, 2048])
                W[l]['wdn'] = din("wdn_%d" % l, [4, DC, 128, 2048])
        HX = {}
        for l in layers:
            n1, mix, wo = ('N1', l) in plan, (('MIX', l) in plan or ('SB', l) in plan or ('GLA', l) in plan), ('WO', l) in plan
            if n1 and mix:
                HX[('hloc', l)] = [dint("hloc_%d_%d" % (l, q), [D, 256], BF16) for q in range(4)]
                HX[('hall', l)] = [dint("hall_%d_%d" % (l, q), [4 * D, 256], BF16) for q in range(4)]
            elif n1:
                HX[('hloc', l)] = dout("h_out_%d" % l, [D, TL], BF16)
            elif mix:
                HX[('hall', l)] = din("hall_%d" % l, [4 * D, TL], BF16)
            if mix and wo:
                HX[('mixloc', l)] = [dint("mixloc_%d_%d" % (l, q), [D, 256], BF16) for q in range(4)]
                HX[('mixall', l)] = [dint("mixall_%d_%d" % (l, q), [4 * D, 256], BF16) for q in range(4)]
            elif mix:
                HX[('mixloc', l)] = dout("mix_out_%d" % l, [D, TL], BF16)
            elif wo:
                HX[('mixin', l)] = din("mix_in_%d" % l, [D, TL], BF16)
        xT_out = dout("xT_out", [D, TL])

        xT = es.enter_context(nc.sbuf_tensor("xT", [128, DC, TL], F32))
        hbuf = es.enter_context(nc.sbuf_tensor("hbuf", [128, DC, TL], BF16))
        epsc = es.enter_context(nc.sbuf_tensor("epsc", [128, 1], F32))
        onec = es.enter_context(nc.sbuf_tensor("onec", [128, 1], F32))
        qgs = es.enter_context(nc.sbuf_tensor("qgs", [128, DEPTH], F32))
        gcols = es.enter_context(nc.sbuf_tensor("gcols", [128, 2 * DEPTH, DC], F32))
        nrms = es.enter_context(nc.sbuf_tensor("nrms", [128, DEPTH, 4], F32))
        wgus = es.enter_context(nc.sbuf_tensor("wgus", [128, DEPTH, 128], F32))
        cbf = es.enter_context(nc.sbuf_tensor("cbf", [128, CO_TRIG], BF16))
        cf32 = es.enter_context(nc.sbuf_tensor("cf32", [128, NCONST - CO_TRIG], F32))
        ARENA = nc.sbuf_bytes_remaining - 64
        ARENA -= ARENA % 4
        arena = es.enter_context(nc.sbuf_tensor("arena", [128, ARENA // 4], F32))
        C.psum = [es.enter_context(nc.psum_tensor("ps%d" % i, [128, 512], F32)) for i in range(8)]
        C.xT, C.hbuf, C.epsc, C.onec, C.qgs = xT, hbuf, epsc, onec, qgs
        C.onesD = cbf[:, CO_ONESD:CO_ONESD + 128]
        C.onesH = cbf[:, CO_ONESH:CO_ONESH + 128]
        C.ones1 = cbf[:, CO_ONES1:CO_ONES1 + 128]
        C.triNeg = cbf[:, CO_TRINEG:CO_TRINEG + 128]
        C.maskd = cbf[:, CO_MASKD:CO_MASKD + 2048].rearrange("p (m t) -> p m t", t=512)
        C.ident = cbf[:, CO_IDENT:CO_IDENT + 128]
        C.triG = cf32[:, 0:128]
        C.triA = cf32[:, 128:256]
        C.maskG2 = cf32[:, 256:512]

        def many(off, n, shape, dt):
            out = []
            for _ in range(n):
                v, off = carve(arena, off, shape, dt)
                out.append(v)
            return out, off

        C.stg, off0 = many(0, 3, [128, 2048], F32)
        off = off0
        C.wb, off = many(off, 3, [128, 2048], BF16)
        C.upT, off = carve(arena, off, [128, 16, TL], BF16)
        C.relu, off = many(off, 2, [128, 512], F32)
        C.sqv, off = many(off, 2, [128, 512], BF16)
        C.lnv, off = carve(arena, off, [128, 512], F32)
        C.rstd, off = carve(arena, off, [128, 512], F32)
        assert off <= ARENA, (off, ARENA)
        off = 2 * 2048 * 4
        C.hT_t, off = carve(arena, off, [128, DC, 512], BF16)
        C.sqv2, off = many(off, 2, [128, 512], BF16)
        C.lnv2, off = carve(arena, off, [128, 512], F32)
        C.rstd2, off = carve(arena, off, [128, 512], F32)
        C.mixo, off = many(off, 2, [128, 2, 512], BF16)
        offm = off
        C.wsb, off = carve(arena, off, [128, 6, 2048], BF16)
        qn4, off = many(off, 4, [128, 512], BF16)
        C.qn2 = [qn4[0:2], qn4[2:4]]
        hT_b, off = carve(arena, off, [128, DC, 512], BF16)
        C.hT2 = [C.hT_t, hT_b]
        C.raw, off = many(off, 1, [128, 512], F32)
        C.E, off = many(off, 2, [128, 512], F32)
        C.Lp, off = many(off, 2, [128, 512], BF16)
        C.Wsb, off = many(off, 2, [128, 512], F32)
        C.A, off = many(off, 2, [128, 512], BF16)
        C.carry, off = carve(arena, off, [128, 512], F32)
        assert off <= ARENA, (off, ARENA)
        hflat = hbuf.rearrange("p a b -> p (a b)")
        C.KT = [hflat[:, h * SEQ:(h + 1) * SEQ] for h in range(2)]
        C.Vall = hflat[:, 2 * SEQ:4 * SEQ].rearrange("p (k d) -> p k d", d=256)
        off = offm
        C.wgl, off = carve(arena, off, [128, 7, 2048], BF16)
        C.gqT, off = carve(arena, off, [128, 512], F32)
        C.gkT, off = carve(arena, off, [128, 512], F32)
        C.sg, off = carve(arena, off, [128, 2, 512], F32)
        C.lrT, off = carve(arena, off, [128, 512], F32)
        C.gv_t, off = carve(arena, off, [128, 4, 256], BF16)
        C.gkt, off = carve(arena, off, [128, 4, 128], F32)
        C.en, off = many(off, 2, [128, 128], F32)
        C.spv, off = many(off, 2, [128, 128], F32)
        C.Eq, off = many(off, 2, [128, 128], F32)
        C.Ek, off = many(off, 2, [128, 128], F32)
        C.Ed, off = many(off, 2, [128, 128], F32)
        C.qe, off = many(off, 2, [128, 128], BF16)
        C.keh, off = many(off, 4, [128, 128], BF16)
        C.kdc, off = many(off, 4, [128, 128], BF16)
        C.scm, off = many(off, 2, [128, 256], BF16)
        C.Sf, off = many(off, 3, [128, 128], F32)
        C.Sbh, off = many(off, 6, [128, 128], BF16)
        C.osq2, off = carve(arena, off, [128, 256], BF16)
        C.lnv3, off = carve(arena, off, [128, 256], F32)
        C.rstd3, off = carve(arena, off, [128, 256], F32)
        C.otmp, off = carve(arena, off, [128, 256], F32)
        assert off <= ARENA, (off, ARENA)

        for dc in range(DC):
            S.add('sp', lambda e, dc=dc: e.dma_start(out=xT[:, dc, :], in_=xT_in[dc * 128:(dc + 1) * 128, :]),
                  writes=[('xT', dc, 0), ('xT', dc, 1)], dma=('x', dc % 4))
        S.add('sp', lambda e: e.dma_start(out=C.stg[0][:, 0:2048], in_=consts_d[:, 0:2048]), writes=['c0'], dma='c0')
        S.add('sp', lambda e: e.dma_start(out=C.stg[1][:, 0:CO_TRIG - 2048], in_=consts_d[:, 2048:CO_TRIG]), writes=['c1'], dma='c1')
        S.add('sp', lambda e: e.dma_start(out=cf32[:, :], in_=consts_d[:, CO_TRIG:NCONST]), writes=['c2'], dma='c2')
        S.add('dve', lambda e: e.tensor_copy(out=cbf[:, 0:2048], in_=C.stg[0][:, 0:2048]), reads=['c0'], writes=['cbf0'])
        S.add('dve', lambda e: e.tensor_copy(out=cbf[:, 2048:CO_TRIG], in_=C.stg[1][:, 0:CO_TRIG - 2048]), reads=['c1'], writes=['cbf1'])
        S.add('dve', lambda e: e.memset(epsc[:, :], EPS), writes=['epsc'])
        S.add('dve', lambda e: e.memset(onec[:, :], 1.0), writes=['onec'])
        for l in layers:
            if 'g1' in W[l]:
                S.add('sp', lambda e, l=l: e.dma_start(out=gcols[:, 2 * l, :], in_=W[l]['g1'][:, :]), writes=[('g', 2 * l)], dma=('g', 2 * l))
            if 'g2' in W[l]:
                S.add('sp', lambda e, l=l: e.dma_start(out=gcols[:, 2 * l + 1, :], in_=W[l]['g2'][:, :]), writes=[('g', 2 * l + 1)], dma=('g', 2 * l + 1))
            if 'nrm' in W[l]:
                S.add('sp', lambda e, l=l: e.dma_start(out=nrms[:, l, :], in_=W[l]['nrm'][:, :]), writes=[('nrm', l)], dma=('nrm', l))
                S.add('sp', lambda e, l=l: e.dma_start(out=wgus[0:17, l, :], in_=W[l]['wgu'][:, :]), writes=[('wgu', l)], dma=('wgu', l))
                S.add('dve', lambda e, l=l: e.tensor_scalar(out=qgs[:, l:l + 1], in0=nrms[:, l, 1:2], scalar1=float(128 ** -0.5), scalar2=None,
                                                           op0=ALU.mult), reads=[('nrm', l)], writes=[('qgs', l)])
        S.barrier()

        for (ph, l) in plan:
            if ph == 'N1':
                phase_norm(S, C, gcols[:, 2 * l, :], "n1")
                if ('MIX', l) in plan:
                    load_pieces(S, C, W[l]['wsb'], 6, C.wsb)
                S.barrier()
                hloc = HX[('hloc', l)]
                if isinstance(hloc, list):
                    hall = HX[('hall', l)]
                    for q in range(4):
                        S.add('sp', lambda e, q=q, hloc=hloc: e.dma_start(out=hloc[q].rearrange("(c p) t -> p c t", p=128),
                                                                         in_=hbuf[:, :, q * 256:(q + 1) * 256]), writes=[('hloc', q)], dma=('hout', q))
                        S.add('pool', lambda e, q=q, hloc=hloc, hall=hall: e.collective_compute(
                            "AllGather", ALU.bypass, replica_groups=GROUPS, ins=[hloc[q].opt()], outs=[hall[q].opt()]),
                            reads=[('hloc', q)], writes=[('hall', q)], dma='cc', inc=1)
                    S.barrier(skip=('cc',), keep=lambda t: isinstance(t, tuple) and t[0] == 'hall')
                else:
                    S.add('sp', lambda e, hloc=hloc: e.dma_start(out=hloc.rearrange("(c p) t -> p c t", p=128), in_=hbuf[:, :, :]), dma='hout')
                    S.barrier()
            elif ph in ('MIX', 'SB', 'GLA'):
                mixloc = HX[('mixloc', l)]
                if ph in ('MIX', 'SB'):
                    phase_sb(S, C, W[l]['wsb'], HX[('hall', l)], mixloc, nrms[:, l, :], qgs[:, l:l + 1], preloaded=(('N1', l) in plan))
                    S.barrier()
                if ph in ('MIX', 'GLA'):
                    gather = None
                    if ('mixall', l) in HX:
                        mixall = HX[('mixall', l)]

                        def gather(q, mixloc=mixloc, mixall=mixall):
                            S.add('pool', lambda e: e.collective_compute("AllGather", ALU.bypass, replica_groups=GROUPS,
                                                                         ins=[mixloc[q].opt()], outs=[mixall[q].opt()]),
                                  reads=[('mixloc', q, 0, a, b) for a in range(2) for b in range(2)], writes=[('mixall', q)], dma='cc', inc=1)
                    phase_gla(S, C, W[l]['wgl'], HX[('hall', l)], mixloc, nrms[:, l, :], wgus[:, l, :], gather)
                    S.barrier(skip=('cc',), keep=lambda t: isinstance(t, tuple) and t[0] == 'mixall')
            elif ph == 'WO':
                if ('mixin', l) in HX:
                    mixin = HX[('mixin', l)]
                    S.add('sp', lambda e, mixin=mixin: e.dma_start(out=hbuf[:, :, :], in_=mixin.rearrange("(c p) t -> p c t", p=128)), dma='mixin')
                    S.barrier()
                else:
                    mixall = HX[('mixall', l)]

                    for q in range(4):
                        for src in range(4):
                            def fetch(e, q=q, src=src, mixall=mixall):
                                if 'rnk' not in C.cache:
                                    C.cache['rnk'] = e.snap(e.partition_id() % 4, min_val=0, max_val=3)
                                rnk = C.cache['rnk']
                                return e.dma_start(out=hbuf[:, src * 4:(src + 1) * 4, q * 256:(q + 1) * 256],
                                                   in_=mixall[q][bass.ds(rnk * 512 + src * D, 512), :].rearrange("(f p) t -> p f t", p=128))
                            S.add('sp', fetch, reads=[('mixall', q)], writes=[('hbufq', src, q)], dma=('mixin', q, src))
                phase_wo(S, C, W[l]['wo'])
                S.barrier()
            elif ph == 'MLP':
                phase_norm(S, C, gcols[:, 2 * l + 1, :], "n2")
                S.barrier()
                phase_mlp(S, C, W[l]['wup'], W[l]['wdn'])
                if (ph, l) != plan[-1]:
                    S.barrier()

        for dc in range(DC):
            S.add('sp', lambda e, dc=dc: e.dma_start(out=xT_out[dc * 128:(dc + 1) * 128, :], in_=xT[:, dc, :]),
                  reads=[('xT', dc, 0), ('xT', dc, 1)], dma=('xo', dc % 4))
        S.barrier()
        S.emit()
    return nc


def tile_w(w):
    K, M = w.shape
    t = w.reshape(K // 128, 128, M // 128, 128)
    t = t.transpose(2, 1, 0, 3)
    return np.ascontiguousarray(t.reshape(M // 128, 128, (K // 128) * 128))


def make_consts():
    c = np.zeros((128, NCONST), np.float32)
    c[:, CO_ONESD:CO_ONESD + 128] = 1.0 / D
    c[:, CO_ONESH:CO_ONESH + 128] = 1.0 / 128
    c[:, CO_ONES1:CO_ONES1 + 128] = 1.0
    p = np.arange(128)[:, None]
    q = np.arange(128)[None, :]
    c[:, CO_TRINEG:CO_TRINEG + 128] = np.where(p >= q, -1.0, 0.0)
    tc = np.arange(512)[None, :]
    for m in range(4):
        c[:, CO_MASKD + m * 512:CO_MASKD + (m + 1) * 512] = np.where(m * 128 + p < tc, 0.0, NEGM)
    c[:, CO_IDENT:CO_IDENT + 128] = np.where(p == q, 1.0, 0.0)
    same = (p // 64) == (q // 64)
    c[:, CO_TRIG:CO_TRIG + 128] = np.where((p <= q) & same, -1.0 / 16, 0.0)
    c[:, CO_TRIA:CO_TRIA + 128] = np.where((p > q) & same, -1.0 / 16, 0.0)
    mg = np.where((p <= q) & same, 1.0, 0.0)
    c[:, CO_MASKG2:CO_MASKG2 + 128] = mg
    c[:, CO_MASKG2 + 128:CO_MASKG2 + 256] = mg
    return c


def piece(wcols):
    return np.ascontiguousarray(wcols.reshape(DC, 128, 128).transpose(1, 0, 2).reshape(128, DC * 128))


def mixer_weights(w_in_l, w_gate_up_l, b_gate_l, gla_out_norm_l, sb_q_norm_l, sb_k_norm_l, sb_out_norm_l, j):
    GQ, GK, GV, GG, LR, SQ, SK, SV = 0, 512, 1024, 2048, 3072, 3088, 4112, 5136
    h0 = 2 * j
    sbp = []
    for base in (SQ, SK, SV):
        for h in (h0, h0 + 1):
            sbp.append(piece(w_in_l[:, base + h * 128:base + (h + 1) * 128]))
    wsb = np.stack(sbp)
    glp = [piece(w_in_l[:, GQ + h0 * 64:GQ + h0 * 64 + 128])]
    for h in (h0, h0 + 1):
        glp.append(piece(w_in_l[:, GG + h * 128:GG + (h + 1) * 128]))
    glp.append(piece(w_in_l[:, GK + h0 * 64:GK + h0 * 64 + 128]))
    for h in (h0, h0 + 1):
        glp.append(piece(w_in_l[:, GV + h * 128:GV + (h + 1) * 128]))
    lr = np.zeros((D, 128), np.float32)
    lr[:, 0:16] = w_in_l[:, LR:LR + 16]
    glp.append(piece(lr))
    wgl = np.stack(glp)
    nrm = np.ascontiguousarray(np.stack([gla_out_norm_l, sb_q_norm_l, sb_k_norm_l, sb_out_norm_l], axis=1)).astype(np.float32)
    wgu = np.zeros((17, 128), np.float32)
    wgu[0:16] = w_gate_up_l[:, h0 * 64:h0 * 64 + 128]
    wgu[16] = b_gate_l[h0 * 64:h0 * 64 + 128]
    return wsb, wgl, nrm, wgu


def wo_tiles(w_o_l):
    blocks = []
    for src in range(4):
        for f in range(4):
            row0 = (2 * src + f) * 128 if f < 2 else 1024 + (2 * src + f - 2) * 128
            blocks.append(w_o_l[row0:row0 + 128, :])
    return tile_w(np.concatenate(blocks, axis=0))


def gcol(g):
    return np.ascontiguousarray(g.reshape(DC, 128).T)


FUSED = True
_PROGRAMS = {}


def _program(plan):
    key = tuple(plan)
    if key not in _PROGRAMS:
        _PROGRAMS[key] = build_program(list(plan))
    return _PROGRAMS[key]


def _layer_inputs(l, j, P):
    wsb, wgl, nrm, wgu = mixer_weights(P["w_in"][l], P["w_gate_up"][l], P["b_gate"][l], P["gla_out_norm"][l],
                                       P["sb_q_norm"][l], P["sb_k_norm"][l], P["sb_out_norm"][l], j)
    return {"wsb_%d" % l: wsb, "wgl_%d" % l: wgl, "nrm_%d" % l: nrm, "wgu_%d" % l: wgu}


def kernel(x, attn_norm, w_in, w_gate_up, b_gate, gla_out_norm, sb_q_norm, sb_k_norm, sb_out_norm, w_o,
           mlp_norm, w_up, w_down):
    P = dict(w_in=np.asarray(w_in, np.float32), w_gate_up=np.asarray(w_gate_up, np.float32), b_gate=np.asarray(b_gate, np.float32),
             gla_out_norm=np.asarray(gla_out_norm, np.float32), sb_q_norm=np.asarray(sb_q_norm, np.float32),
             sb_k_norm=np.asarray(sb_k_norm, np.float32), sb_out_norm=np.asarray(sb_out_norm, np.float32))
    x = np.asarray(x, np.float32)
    attn_norm = np.asarray(attn_norm, np.float32)
    mlp_norm = np.asarray(mlp_norm, np.float32)
    w_o = np.asarray(w_o, np.float32)
    w_up = np.asarray(w_up, np.float32)
    w_down = np.asarray(w_down, np.float32)
    consts = make_consts()
    cores = list(range(NCORES))
    xT = [np.ascontiguousarray(x[c // 4].reshape(4, 4, 256, D)[:, c % 4].reshape(TL, D).T) for c in cores]
    shared = {}
    for l in range(DEPTH):
        shared[l] = {"g1_%d" % l: gcol(attn_norm[l]), "g2_%d" % l: gcol(mlp_norm[l]), "wo_%d" % l: wo_tiles(w_o[l]),
                     "wup_%d" % l: tile_w(w_up[l]),
                     "wdn_%d" % l: np.stack([tile_w(w_down[l][fg * 2048:(fg + 1) * 2048]) for fg in range(4)])}
    mixw = {(l, j): _layer_inputs(l, j, P) for l in range(DEPTH) for j in range(4)}

    def pick(d, names):
        return {k: d[k] for k in names}

    if FUSED:
        plan = []
        for l in range(DEPTH):
            plan += [('N1', l), ('MIX', l), ('WO', l), ('MLP', l)]
        nc = _program(plan)
        ins = []
        for c in cores:
            m = {"xT_in": xT[c], "consts": consts}
            for l in range(DEPTH):
                m.update(shared[l])
                m.update(mixw[(l, c % 4)])
            ins.append(m)
        res = run_bass_kernel_spmd(nc, ins, core_ids=cores)
        outT = [np.asarray(res.results[c]["xT_out"]) for c in cores]
    else:
        cur = xT
        h = None
        for l in range(DEPTH):
            if l == 0:
                nc = _program([('N1', 0)])
                ins = [dict(xT_in=cur[c], consts=consts, **pick(shared[0], ["g1_0"])) for c in cores]
                res = run_bass_kernel_spmd(nc, ins, core_ids=cores)
                h = [np.asarray(res.results[c]["h_out_0"]) for c in cores]
            hall = [np.concatenate([h[(c // 4) * 4 + r] for r in range(4)], axis=0) for c in cores]
            nc = _program([('MIX', l)])
            ins = [dict(xT_in=cur[c], consts=consts, **{"hall_%d" % l: hall[c]}, **mixw[(l, c % 4)]) for c in cores]
            res = run_bass_kernel_spmd(nc, ins, core_ids=cores)
            mo = [np.asarray(res.results[c]["mix_out_%d" % l]) for c in cores]
            mix_in = [np.concatenate([mo[(c // 4) * 4 + src][(c % 4) * 512:(c % 4 + 1) * 512] for src in range(4)], axis=0) for c in cores]
            plan = [('WO', l), ('MLP', l)] + ([('N1', l + 1)] if l + 1 < DEPTH else [])
            nc = _program(plan)
            names = ["g2_%d" % l, "wo_%d" % l, "wup_%d" % l, "wdn_%d" % l]
            ins = []
            for c in cores:
                m = dict(xT_in=cur[c], consts=consts, **{"mix_in_%d" % l: mix_in[c]}, **pick(shared[l], names))
                if l + 1 < DEPTH:
                    m["g1_%d" % (l + 1)] = shared[l + 1]["g1_%d" % (l + 1)]
                ins.append(m)
            res = run_bass_kernel_spmd(nc, ins, core_ids=cores)
            cur = [np.asarray(res.results[c]["xT_out"]) for c in cores]
            if l + 1 < DEPTH:
                h = [np.asarray(res.results[c]["h_out_%d" % (l + 1)]) for c in cores]
        outT = cur
    out = np.empty((2, SEQ, D), np.float32)
    for c in cores:
        out[c // 4].reshape(4, 4, 256, D)[:, c % 4] = outT[c].T.reshape(4, 256, D)
    return out
```
